# Optimizing a Trainium2 kernel written in Bass

```python
import math
import jax, jax.numpy as jnp
from jax import lax
import numpy as np

D_MODEL = 1024
BATCH = 4
SEQ = 8192
DEPTH = 1

HEAD_DIM = 64
HEADS_PER_GROUP = 4
DILATED_GROUPS = ((128, 1), (512, 4), (2048, 16))
N_ATTN_GROUPS = len(DILATED_GROUPS)
N_ATTN_HEADS = N_ATTN_GROUPS * HEADS_PER_GROUP
ATTN_WIDTH = N_ATTN_HEADS * HEAD_DIM
ATTN_OUT_WIDTH = HEADS_PER_GROUP * HEAD_DIM
BLOCK = 128
N_BUCKETS = 32
MAX_DISTANCE = 2048
NEG_INF = -1e30
SSM_GROUP = 16
SSM_WIDTH = 512
SSM_GROUPS = SSM_WIDTH // SSM_GROUP
SSM_STATE = 64
DT_MIN = 1e-3
DT_MAX = 1e-1
D_FF = 2816
EPS = 1e-6
IN_WIDTH = 3 * ATTN_WIDTH + SSM_WIDTH + 2 * D_MODEL

kernel_name = "hybrid_dilated_attn_s5_macaron"


def rms_norm(x, g):
    xf = x.astype(jnp.float32)
    y = xf * lax.rsqrt(jnp.mean(xf * xf, axis=-1, keepdims=True) + EPS)
    return (y * g.astype(jnp.float32)).astype(x.dtype)


def swiglu(h, w_gate, w_up, w_down):
    return (jax.nn.silu(h @ w_gate) * (h @ w_up)) @ w_down


def t5_bucket(dist):
    max_exact = N_BUCKETS // 2
    d = jnp.maximum(dist, 1).astype(jnp.float32)
    large = max_exact + (jnp.log(d / max_exact) / math.log(MAX_DISTANCE / max_exact)
                         * (N_BUCKETS - max_exact)).astype(jnp.int32)
    large = jnp.minimum(large, N_BUCKETS - 1)
    return jnp.where(dist < max_exact, dist, large)


def dilated_group_attention(q, k, v, bias_table_g, window, dilation):
    B, L, H, Dh = q.shape
    M = L // dilation
    n_steps = window // dilation
    nb = -(-M // BLOCK)
    Mp = nb * BLOCK

    def to_sub(t):
        t = t.reshape(B, M, dilation, H, Dh).transpose(0, 2, 1, 3, 4).reshape(B * dilation, M, H, Dh)
        t = jnp.pad(t, ((0, 0), (0, Mp - M), (0, 0), (0, 0)))
        return t.reshape(B * dilation, nb, BLOCK, H, Dh)

    def with_prev(t):
        prev = jnp.pad(t, ((0, 0), (1, 0), (0, 0), (0, 0), (0, 0)))[:, :-1]
        return jnp.concatenate([prev, t], axis=2)

    qs = to_sub(q).astype(jnp.float32)
    kb = with_prev(to_sub(k)).astype(jnp.float32)
    vb = with_prev(to_sub(v)).astype(jnp.float32)

    qi = jnp.arange(BLOCK)[:, None]
    kj = jnp.arange(2 * BLOCK)[None, :]
    steps = qi + BLOCK - kj
    band = (steps >= 0) & (steps <= n_steps)
    first_ok = (jnp.arange(nb)[:, None] > 0) | (kj >= BLOCK)
    mask = band[None] & first_ok[:, None, :]
    bucket = t5_bucket(jnp.maximum(steps, 0) * dilation)
    bias = bias_table_g.astype(jnp.float32)[bucket].transpose(2, 0, 1)

    logits = jnp.einsum('bnqhd,bnkhd->bhnqk', qs, kb) + bias[None, :, None]
    logits = jnp.where(mask[None, None], logits, NEG_INF)
    m = jnp.max(logits, axis=-1, keepdims=True)
    p = jnp.exp(logits - m)
    denom = jnp.sum(p, axis=-1)
    o = jnp.einsum('bhnqk,bnkhd->bnqhd', p, vb) / denom.transpose(0, 2, 3, 1)[..., None]
    lse = m[..., 0] + jnp.log(denom)

    o = o.reshape(B * dilation, Mp, H, Dh)[:, :M]
    o = o.reshape(B, dilation, M, H, Dh).transpose(0, 2, 1, 3, 4).reshape(B, L, H, Dh)
    lse = lse.transpose(0, 2, 3, 1).reshape(B * dilation, Mp, H)[:, :M]
    lse = lse.reshape(B, dilation, M, H).transpose(0, 2, 1, 3).reshape(B, L, H)
    return o, lse


def s5_mixer(u, a_re, a_im, log_dt, b_re, b_im, c_re, c_im, d_skip):
    B, L, _ = u.shape
    uf = u.astype(jnp.float32).reshape(B, L, SSM_GROUPS, SSM_GROUP)
    lam_re = a_re.astype(jnp.float32)
    lam_im = a_im.astype(jnp.float32)
    dt = jnp.exp(log_dt.astype(jnp.float32))[:, None]
    mag = jnp.exp(lam_re * dt)
    ab_re = mag * jnp.cos(lam_im * dt)
    ab_im = mag * jnp.sin(lam_im * dt)
    den = lam_re * lam_re + lam_im * lam_im
    xr = ab_re - 1.0
    coef_re = (xr * lam_re + ab_im * lam_im) / den
    coef_im = (ab_im * lam_re - xr * lam_im) / den
    br = b_re.astype(jnp.float32)
    bi = b_im.astype(jnp.float32)
    bb_re = coef_re[..., None] * br - coef_im[..., None] * bi
    bb_im = coef_re[..., None] * bi + coef_im[..., None] * br
    bu_re = jnp.einsum('gnc,blgc->lbgn', bb_re, uf)
    bu_im = jnp.einsum('gnc,blgc->lbgn', bb_im, uf)
    a_seq_re = jnp.broadcast_to(ab_re[None, None], (L, 1, SSM_GROUPS, SSM_STATE))
    a_seq_im = jnp.broadcast_to(ab_im[None, None], (L, 1, SSM_GROUPS, SSM_STATE))

    def combine(left, right):
        al_re, al_im, bl_re, bl_im = left
        ar_re, ar_im, brr, bri = right
        return (al_re * ar_re - al_im * ar_im,
                al_re * ar_im + al_im * ar_re,
                ar_re * bl_re - ar_im * bl_im + brr,
                ar_re * bl_im + ar_im * bl_re + bri)

    _, _, s_re, s_im = lax.associative_scan(combine, (a_seq_re, a_seq_im, bu_re, bu_im), axis=0)
    y = (jnp.einsum('gcn,lbgn->blgc', c_re.astype(jnp.float32), s_re)
         - jnp.einsum('gcn,lbgn->blgc', c_im.astype(jnp.float32), s_im)
         + d_skip.astype(jnp.float32).reshape(SSM_GROUPS, SSM_GROUP) * uf)
    return y.reshape(B, L, SSM_WIDTH).astype(u.dtype)


def setup_inputs(seed: int = 0) -> dict:
    key = jax.random.key(seed)
    ks = iter(jax.random.split(key, 32))
    f32 = jnp.float32

    def nrm(shape, scale):
        return jax.random.normal(next(ks), shape, f32) * scale

    def gain(shape):
        return 1.0 + 0.05 * jax.random.normal(next(ks), shape, f32)

    L_ = DEPTH
    n_idx = jnp.arange(SSM_STATE, dtype=f32)
    return {
        "x": jax.random.normal(next(ks), (BATCH, SEQ, D_MODEL), f32),
        "ffn1_norm": gain((L_, D_MODEL)),
        "ffn1_w_gate": nrm((L_, D_MODEL, D_FF), D_MODEL ** -0.5),
        "ffn1_w_up": nrm((L_, D_MODEL, D_FF), D_MODEL ** -0.5),
        "ffn1_w_down": nrm((L_, D_FF, D_MODEL), D_FF ** -0.5),
        "mix_norm": gain((L_, D_MODEL)),
        "w_in": nrm((L_, D_MODEL, IN_WIDTH), D_MODEL ** -0.5),
        "gate_bias": nrm((L_, 2 * D_MODEL), 0.1),
        "rel_bias_table": nrm((N_BUCKETS, N_ATTN_HEADS), 0.5),
        "ssm_a_re": -0.5 + nrm((L_, SSM_GROUPS, SSM_STATE), 0.01),
        "ssm_a_im": math.pi * n_idx + nrm((L_, SSM_GROUPS, SSM_STATE), 0.01),
        "ssm_log_dt": jax.random.uniform(next(ks), (L_, SSM_GROUPS), f32,
                                         math.log(DT_MIN), math.log(DT_MAX)),
        "ssm_b_re": nrm((L_, SSM_GROUPS, SSM_STATE, SSM_GROUP), (2 * SSM_GROUP) ** -0.5),
        "ssm_b_im": nrm((L_, SSM_GROUPS, SSM_STATE, SSM_GROUP), (2 * SSM_GROUP) ** -0.5),
        "ssm_c_re": nrm((L_, SSM_GROUPS, SSM_GROUP, SSM_STATE), (2 * SSM_STATE) ** -0.5),
        "ssm_c_im": nrm((L_, SSM_GROUPS, SSM_GROUP, SSM_STATE), (2 * SSM_STATE) ** -0.5),
        "ssm_d": nrm((L_, SSM_WIDTH), 1.0),
        "ssm_w_glu": nrm((L_, SSM_WIDTH, 2 * SSM_WIDTH), SSM_WIDTH ** -0.5),
        "w_attn_branch": nrm((L_, ATTN_OUT_WIDTH, D_MODEL), ATTN_OUT_WIDTH ** -0.5),
        "w_ssm_branch": nrm((L_, SSM_WIDTH, D_MODEL), SSM_WIDTH ** -0.5),
        "w_out": nrm((L_, D_MODEL, D_MODEL), D_MODEL ** -0.5),
        "ffn2_norm": gain((L_, D_MODEL)),
        "ffn2_w_gate": nrm((L_, D_MODEL, D_FF), D_MODEL ** -0.5),
        "ffn2_w_up": nrm((L_, D_MODEL, D_FF), D_MODEL ** -0.5),
        "ffn2_w_down": nrm((L_, D_FF, D_MODEL), D_FF ** -0.5),
        "final_norm": gain((D_MODEL,)),
    }


def reference(x, ffn1_norm, ffn1_w_gate, ffn1_w_up, ffn1_w_down, mix_norm, w_in, gate_bias,
              rel_bias_table, ssm_a_re, ssm_a_im, ssm_log_dt, ssm_b_re, ssm_b_im, ssm_c_re,
              ssm_c_im, ssm_d, ssm_w_glu, w_attn_branch, w_ssm_branch, w_out, ffn2_norm,
              ffn2_w_gate, ffn2_w_up, ffn2_w_down, final_norm):
    B, L, _ = x.shape
    scale = HEAD_DIM ** -0.5
    for l in range(DEPTH):
        x = x + 0.5 * swiglu(rms_norm(x, ffn1_norm[l]), ffn1_w_gate[l], ffn1_w_up[l], ffn1_w_down[l])

        h = rms_norm(x, mix_norm[l])
        z = h @ w_in[l]
        c0 = ATTN_WIDTH
        q = z[..., :c0].reshape(B, L, N_ATTN_HEADS, HEAD_DIM) * scale
        k = z[..., c0:2 * c0].reshape(B, L, N_ATTN_HEADS, HEAD_DIM)
        v = z[..., 2 * c0:3 * c0].reshape(B, L, N_ATTN_HEADS, HEAD_DIM)
        c1 = 3 * c0
        u = z[..., c1:c1 + SSM_WIDTH]
        c2 = c1 + SSM_WIDTH
        g_attn = jax.nn.sigmoid(z[..., c2:c2 + D_MODEL] + gate_bias[l, :D_MODEL])
        g_ssm = jax.nn.sigmoid(z[..., c2 + D_MODEL:] + gate_bias[l, D_MODEL:])

        outs, lses = [], []
        for g, (window, dilation) in enumerate(DILATED_GROUPS):
            hs = slice(g * HEADS_PER_GROUP, (g + 1) * HEADS_PER_GROUP)
            o_g, lse_g = dilated_group_attention(q[:, :, hs], k[:, :, hs], v[:, :, hs],
                                                 rel_bias_table[:, hs], window, dilation)
            outs.append(o_g)
            lses.append(lse_g)
        o_stack = jnp.stack(outs, axis=2)
        w_grp = jax.nn.softmax(jnp.stack(lses, axis=2), axis=2)
        o_attn = jnp.sum(w_grp[..., None] * o_stack, axis=2).reshape(B, L, ATTN_OUT_WIDTH)
        y_attn = o_attn.astype(x.dtype) @ w_attn_branch[l]

        y_s = jax.nn.gelu(s5_mixer(u, ssm_a_re[l], ssm_a_im[l], ssm_log_dt[l], ssm_b_re[l],
                                   ssm_b_im[l], ssm_c_re[l], ssm_c_im[l], ssm_d[l]))
        glu = y_s @ ssm_w_glu[l]
        y_s = glu[..., :SSM_WIDTH] * jax.nn.sigmoid(glu[..., SSM_WIDTH:])
        y_ssm = y_s @ w_ssm_branch[l]

        x = x + (g_attn * y_attn + g_ssm * y_ssm) @ w_out[l]

        x = x + 0.5 * swiglu(rms_norm(x, ffn2_norm[l]), ffn2_w_gate[l], ffn2_w_up[l], ffn2_w_down[l])
    return rms_norm(x, final_norm)
```

```python
import contextlib
import math
import numpy as np
import concourse.bass as bass
import concourse.mybir as mybir
from concourse.bass_utils import run_bass_kernel_spmd

F32 = mybir.dt.float32
BF16 = mybir.dt.bfloat16
AF = mybir.ActivationFunctionType
ALU = mybir.AluOpType

D = 1024
DFF = 2816
NJ = DFF // 128
T = 512
NS = T // 128
EPS = 1e-6
Q = 8
NCH = T // Q
ENGS = ["sync", "gpsimd", "scalar", "vector", "tensor"]
DILS = (1, 4, 16)
MV = list(range(Q + 1)) + [Q * 2 ** i for i in range(7)] + [Q * (i + 1) for i in range(7)]


class Sched:
    def __init__(self, nc, stack):
        self.nc = nc
        self.stack = stack
        self.ops = {e: [] for e in ENGS}
        self.state = {}
        self.segs = []
        self.dma_count = {}
        self.final_keys = set()
        self.enabled = True

    def _ovl(self, k):
        if isinstance(k, tuple) and k and k[0] == "@":
            lo, hi = k[1], k[2]
            for b in (lo, hi):
                for (slo, shi) in list(self.segs):
                    if slo < b < shi:
                        st = self.state.pop(("@", slo, shi))
                        self.segs.remove((slo, shi))
                        for a_, b_ in ((slo, b), (b, shi)):
                            self.segs.append((a_, b_))
                            self.state[("@", a_, b_)] = [st[0], list(st[1])]
                        break
            covered = sorted((slo, shi) for (slo, shi) in self.segs if slo < hi and lo < shi)
            cur = lo
            for slo, shi in covered:
                if slo > cur:
                    self.segs.append((cur, slo))
                    self.state[("@", cur, slo)] = [None, []]
                cur = max(cur, shi)
            if cur < hi:
                self.segs.append((cur, hi))
                self.state[("@", cur, hi)] = [None, []]
            return [("@", slo, shi) for (slo, shi) in self.segs if lo <= slo and shi <= hi]
        if k not in self.state:
            self.state[k] = [None, []]
        return [k]

    def op(self, eng, fn, reads=(), writes=(), dma_key=None, final=False):
        if not self.enabled:
            return None
        rec = dict(eng=eng, fn=fn, deps=[], idx=len(self.ops[eng]), dma_key=dma_key, marked=False)
        seen = set()

        def add(d):
            if d is not None and id(d) not in seen:
                seen.add(id(d))
                rec["deps"].append(d)

        rk = [k2 for k in reads for k2 in self._ovl(k)]
        wk = [k2 for k in writes for k2 in self._ovl(k)]
        for k in rk:
            add(self.state[k][0])
        for k in wk:
            add(self.state[k][0])
            for r in self.state[k][1]:
                add(r)
        for k in wk:
            self.state[k][0] = rec
            self.state[k][1] = []
        wset = set(wk)
        for k in rk:
            if k not in wset:
                self.state[k][1].append(rec)
        best = {}
        for d_ in rec["deps"]:
            kk = ("dma", d_["dma_key"]) if d_["dma_key"] is not None else ("eng", d_["eng"])
            cur = best.get(kk)
            v = d_["dma_req"] if d_["dma_key"] is not None else d_["idx"]
            if cur is None or v > (cur["dma_req"] if cur["dma_key"] is not None else cur["idx"]):
                best[kk] = d_
        rec["deps"] = list(best.values())
        if dma_key is not None:
            c = self.dma_count.get(dma_key, 0) + 1
            self.dma_count[dma_key] = c
            rec["dma_req"] = 16 * c
            if final:
                self.final_keys.add(dma_key)
        self.ops[eng].append(rec)
        return rec

    @staticmethod
    def _needs_sync(d, rec):
        if d["dma_key"] is not None:
            return True
        if d["eng"] != rec["eng"]:
            return True
        if rec["eng"] == "tensor":
            return False
        return rec["idx"] - d["idx"] <= 3

    def emit(self):
        nc, stack = self.nc, self.stack
        ROT = 30000
        for e in ENGS:
            for rec in self.ops[e]:
                for d in rec["deps"]:
                    if d["dma_key"] is None and self._needs_sync(d, rec):
                        d["marked"] = True
        eng_sems = {}
        for e in ENGS:
            n = 0
            for rec in self.ops[e]:
                if rec["marked"]:
                    si, v = divmod(n, ROT)
                    key = (e, si)
                    if key not in eng_sems:
                        eng_sems[key] = stack.enter_context(nc.semaphore(f"s_{e}_{si}"))
                    rec["mark"] = (eng_sems[key], v + 1)
                    n += 1
        dma_sems = {}
        for i, k in enumerate(self.dma_count):
            dma_sems[k] = stack.enter_context(nc.semaphore(f"d_{i}"))

        def emit_engine(e, eng):
            waited = {}
            for rec in self.ops[e]:
                need = {}
                for d in rec["deps"]:
                    if not self._needs_sync(d, rec):
                        continue
                    if d["dma_key"] is not None:
                        k = d["dma_key"]
                        sem = dma_sems[k]
                        val = 16 * self.dma_count[k] if k in self.final_keys else d["dma_req"]
                    else:
                        sem, val = d["mark"]
                    if need.get(id(sem), (None, 0))[1] < val:
                        need[id(sem)] = (sem, val)
                for sid, (sem, val) in need.items():
                    if waited.get(sid, 0) < val:
                        eng.wait_ge(sem, val)
                        waited[sid] = val
                ins = rec["fn"](eng)
                if rec["dma_key"] is not None:
                    ins.then_inc(dma_sems[rec["dma_key"]], 16)
                elif rec["marked"]:
                    ins.then_inc(rec["mark"][0], 1)

        with nc.Block() as block:
            @block.sync
            def _(eng):
                emit_engine("sync", eng)

            @block.gpsimd
            def _(eng):
                emit_engine("gpsimd", eng)

            @block.scalar
            def _(eng):
                emit_engine("scalar", eng)

            @block.vector
            def _(eng):
                emit_engine("vector", eng)

            @block.tensor
            def _(eng):
                emit_engine("tensor", eng)


INPUT_SPECS = [
    ("x", [8192, 1024]),
    ("ffn1_norm", [1, 1024]), ("mix_norm", [1, 1024]), ("ffn2_norm", [1, 1024]), ("final_norm", [1024]),
    ("gate_bias", [1, 2048]), ("rel_bias_table", [32, 12]), ("ssm_a_re", [1, 32, 64]),
    ("ssm_a_im", [1, 32, 64]), ("ssm_log_dt", [1, 32]), ("ssm_b_re", [1, 32, 64, 16]),
    ("ssm_b_im", [1, 32, 64, 16]), ("ssm_c_re", [1, 32, 16, 64]), ("ssm_c_im", [1, 32, 16, 64]),
    ("ssm_d", [1, 512]),
    ("h_wgu_ffn1", [22, 128, 8, 256]), ("h_wd_ffn1", [2816, 1024]), ("h_wgu_ffn2", [22, 128, 8, 256]), ("h_wd_ffn2", [2816, 1024]),
    ("h_wqk", [12, 128, 8, 128]), ("h_wv", [3, 128, 8, 256]), ("h_wu", [4, 128, 8, 128]), ("h_wg", [16, 128, 8, 128]),
    ("h_wglu", [8, 128, 4, 128]), ("h_wab", [8, 128, 2, 128]), ("h_wsb", [8, 128, 4, 128]), ("h_wo", [1024, 1024]),
    ("c_ident", [128, 128]), ("c_flip", [128, 128]), ("c_bmask", [128, 128]), ("c_ohv", [3, 2, 33, 255]),
    ("c_valid", [128, 64]), ("c_mvec", [128, len(MV), 16]),
]


def host_weight_layouts(inputs):
    f32 = lambda a: np.asarray(a, dtype=np.float32)

    def tiles(w, kt, ncol, width):
        return np.ascontiguousarray(w.reshape(kt, 128, ncol, width).transpose(2, 1, 0, 3))

    out = {}
    for f in ("ffn1", "ffn2"):
        g = tiles(f32(inputs[f + "_w_gate"])[0], 8, 22, 128)
        u = tiles(f32(inputs[f + "_w_up"])[0], 8, 22, 128)
        out["h_wgu_" + f] = np.ascontiguousarray(np.concatenate([g, u], axis=3))
        out["h_wd_" + f] = np.ascontiguousarray(f32(inputs[f + "_w_down"])[0])
    win = f32(inputs["w_in"])[0]
    out["h_wqk"] = tiles(win[:, 0:1536], 8, 12, 128)
    out["h_wv"] = tiles(win[:, 1536:2304], 8, 3, 256)
    out["h_wu"] = tiles(win[:, 2304:2816], 8, 4, 128)
    out["h_wg"] = tiles(win[:, 2816:4864], 8, 16, 128)
    out["h_wglu"] = tiles(f32(inputs["ssm_w_glu"])[0], 4, 8, 128)
    out["h_wab"] = tiles(f32(inputs["w_attn_branch"])[0], 2, 8, 128)
    out["h_wsb"] = tiles(f32(inputs["w_ssm_branch"])[0], 4, 8, 128)
    out["h_wo"] = np.ascontiguousarray(f32(inputs["w_out"])[0])
    return out


def _t5_bucket(dist):
    if dist < 16:
        return int(dist)
    d = np.float32(max(dist, 1))
    v = np.log(d / np.float32(16)) / np.float32(math.log(2048 / 16)) * np.float32(16)
    return int(min(16 + int(np.float32(v)), 31))


def host_consts():
    ident = np.eye(128, dtype=np.float32)
    flip = np.ascontiguousarray(ident[::-1])
    r = np.arange(128)
    bmask = (r[:, None] // 16 == r[None, :] // 16).astype(np.float32)
    ohv = np.zeros((3, 2, 33, 255), np.float32)
    for g, d in enumerate(DILS):
        for pc in range(2):
            for n in range(255):
                if pc == 1:
                    valid, steps = n >= 127, n - 127
                else:
                    valid, steps = n <= 127, n + 1
                b = _t5_bucket(steps * d) if valid else 32
                ohv[g, pc, b, n] = 1.0
    mvec = np.ascontiguousarray(np.broadcast_to(np.asarray(MV, np.float32)[None, :, None], (128, len(MV), 16)))
    return {"c_ident": ident, "c_flip": flip, "c_bmask": bmask, "c_ohv": ohv, "c_mvec": mvec}


def build_nc(n_tiles=16, stages=("ffn1", "mix", "ffn2"), stop_after=None, n_prefix=8):
    nc = bass.Bass("TRN2", target_bir_lowering=False)
    I = {}
    for name, shape in INPUT_SPECS:
        shp = list(shape)
        if name == "x":
            shp = [n_tiles * T, 1024]
        I[name] = nc.dram_tensor(name, shp, F32, kind="ExternalInput").ap()
    out = nc.dram_tensor("out", [(n_tiles - n_prefix) * T, 1024], F32, kind="ExternalOutput").ap()

    with contextlib.ExitStack() as stack:
        def sb(name, shape, dt):
            return stack.enter_context(nc.sbuf_tensor(name, shape, dt))

        def ps(name, shape, dt):
            return stack.enter_context(nc.psum_tensor(name, shape, dt))

        def dram(name, shape, dt):
            t = nc.dram_tensor(name, shape, dt)
            return t.ap() if hasattr(t, "ap") else t

        S = Sched(nc, stack)
        PI = math.pi

        def CP(name):
            if name == stop_after:
                S.enabled = False

        Wgu = {f: dram(f"wgu_{f}", [NJ, 128, 8, 256], BF16) for f in ("ffn1", "ffn2")}
        Wd = {f: dram(f"wd_{f}", [DFF, D], BF16) for f in ("ffn1", "ffn2")}
        Wqk = dram("wqk", [12, 128, 8, 128], BF16)
        Wv = dram("wv", [3, 128, 8, 256], BF16)
        Wu = dram("wu", [4, 128, 8, 128], BF16)
        Wg = dram("wg", [16, 128, 8, 128], BF16)
        Wglu = dram("wglu", [8, 128, 4, 128], BF16)
        Wab = dram("wab", [8, 128, 2, 128], BF16)
        Wsb = dram("wsb", [8, 128, 4, 128], BF16)
        Wo = dram("wo", [D, D], BF16)
        evecD = dram("evecd", [4, 6, 256], F32)

        ARENA = 58 * 1024
        arena = sb("arena", [128, ARENA // 2], BF16)

        class AT:
            def __init__(self, off, shape, dt):
                self.off, self.shape, self.dt = off, shape, dt
                self.esz = 4 if dt == F32 else 2
                n = int(np.prod(shape[1:]))
                self.nbytes = n * self.esz
                v = arena[0:shape[0], off // 2: off // 2 + self.nbytes // 2]
                if dt == F32:
                    v = v.bitcast(F32)
                if len(shape) == 3:
                    v = v.rearrange("p (a b) -> p a b", a=shape[1])
                elif len(shape) == 4:
                    v = v.rearrange("p (a b c) -> p a b c", a=shape[1], b=shape[2])
                self.v = v

            def key(self):
                return ("@", self.off, self.off + self.nbytes)

            def sub(self, i, n):
                w = self.nbytes // n
                return ("@", self.off + i * w, self.off + (i + 1) * w)

        o = 0
        hT = AT(o, [128, NJ, T], BF16); o += hT.nbytes
        wd_sb = [AT(o + i * 11264, [128, NJ, 256], BF16) for i in range(2)]; o += 2 * 11264
        wgu_sb = [AT(o + i * 4096, [128, 8, 256], BF16) for i in range(3)]; o += 3 * 4096
        assert o <= ARENA, o
        ot = [AT(s * 4096, [128, D], F32) for s in range(NS)]
        grow_final = AT(22528, [128, D], F32)
        o = 0
        Hs = AT(o, [128, NCH + 1, 48], F32); o += 12544
        wv_sb = [AT(o + i * 4096, [128, 8, 256], BF16) for i in range(2)]
        mT = AT(o, [128, 8, T], BF16); o += 8192
        QT = AT(o, [128, 6, T], BF16)
        wo_sb = [AT(o + i * 4096, [128, 8, 256], BF16) for i in range(2)]
        o += QT.nbytes
        Sb = [AT(o + i * 2048, [128, 16, NCH], BF16) for i in range(2)]; o += 4096
        uT = AT(o, [128, 4, T], BF16); o += uT.nbytes
        geT = AT(o, [128, 4, T], BF16); o += geT.nbytes
        ysb = [AT(o + i * 2048, [128, T], F32) for i in range(2)]; o += 4096
        ys2T = AT(o, [128, 4, T], BF16); o += 4096
        oT = AT(o, [128, 2, T], BF16); o += 2048
        PT = [AT(o + i * 1024, [128, T], BF16) for i in range(2)]; o += 2048
        wqk_sb = [AT(o + i * 2048, [128, 8, 128], BF16) for i in range(2)]; o += 4096
        bs1 = AT(o, [128, 8, 32], F32); o += 1024
        bs2 = AT(o, [128, 8, 32], F32); o += 1024
        assert o <= ARENA, o
        wg_sb = [AT(wqk_sb[0].off, [128, 8, 128], BF16), AT(wqk_sb[1].off, [128, 8, 128], BF16),
                 AT(uT.off, [128, 8, 128], BF16), AT(uT.off + 2048, [128, 8, 128], BF16)]
        wglu_sb = [AT(QT.off + i * 1024, [128, 4, 128], BF16) for i in range(4)]
        o = 0
        zt = {}
        for nm in ("bbz_re", "bbz_im", "ctz_re", "ctz_im", "tmpz", "tmpz2"):
            zt[nm] = AT(o, [128, 16, 2, 16], F32); o += 2048
        for nm in ("abz_re0", "abz_im0", "caz_re0", "caz_im0", "abz_re1", "abz_im1", "caz_re1", "caz_im1", "ctzb_re", "nctzb_im"):
            zt[nm] = AT(o, [128, 16, 2, 16], BF16); o += 1024
        for nm in ("b_re", "b_im", "ct_re", "ct_im", "tA", "tB"):
            zt[nm] = AT(o, [128, 16, 16], F32); o += 1024
        cnat = [AT(o + i * 512, [128, 128], F32) for i in range(4)]; o += 2048
        cnatb = [AT(o + i * 256, [128, 128], BF16) for i in range(4)]; o += 1024
        Rh = [AT(o + i * 512, [128, 128], F32) for i in range(4)]; o += 4 * 512
        Rhb = [AT(o + i * 256, [128, 128], BF16) for i in range(4)]; o += 4 * 256
        ohv_sb = AT(o, [64, 6, 255], F32); o += 6 * 255 * 4
        ohvb = AT(o, [64, 6, 256], BF16); o += 6 * 256 * 2
        evec_sb = AT(o, [4, 6, 256], F32); o += 6 * 256 * 4
        pw = {}
        for nm in ("mv", "mag", "a1", "a2", "k"):
            pw[nm] = AT(o, [128, len(MV), 16], F32); o += len(MV) * 64
        assert o <= ARENA, o

        xt = [sb(f"xt{s}", [128, D], F32) for s in range(NS)]
        xn = [sb(f"xn{i}", [128, D], BF16) for i in range(1)]
        junk = sb("junk", [128, D], BF16)
        xT = sb("xT", [128, 8, T], BF16)
        sg = [sb(f"sg{i}", [128, T], F32) for i in range(2)]
        gT = {n: sb(f"gT_{n}", [128, 8], F32) for n in ("ffn1", "mix", "ffn2")}
        ss = sb("ss", [128, 8], F32)
        ssq = sb("ssq", [128, 16], F32)
        junkq = sb("junkq", [128, 256], BF16)
        rstd = sb("rstd", [128, 8], F32)
        idf = sb("idf", [128, 128], F32)
        idb = sb("idb", [128, 128], BF16)
        flipf = sb("flipf", [128, 128], F32)
        flipb = sb("flipb", [128, 128], BF16)
        tabb = sb("tabb", [64, 12], BF16)
        bmask = sb("bmask", [128, 128], F32)
        ones_bf = sb("ones_bf", [128, 64], BF16)
        valid_f = sb("valid_f", [128, 64], F32)
        valid_bf = sb("valid_bf", [128, 64], BF16)
        KT = {0: [sb(f"kt0_{i}", [128, 2 * T], BF16) for i in range(2)],
              1: [sb(f"kt1_{i}", [128, 2 * T], BF16) for i in range(2)],
              2: [sb(f"kt2_{i}", [128, 4096], BF16) for i in range(2)]}
        VR = {0: sb("v0", [128, 8, 256], BF16), 1: sb("v1", [128, 8, 256], BF16), 2: sb("v2", [128, 32, 256], BF16)}
        Ecat = [[sb(f"ecat{g}_{h}", [128, 256], BF16) for h in range(4)] for g in range(3)]
        Gt = sb("Gt", [128, 4, Q, 128], BF16)
        Bt = sb("Bt", [128, 4, Q, 2, 128], BF16)
        Ct = sb("Ct", [128, 16, Q, 2, 32], BF16)
        AR2 = sb("AR2", [128, 32], F32)
        AI2 = sb("AI2", [128, 32], F32)
        APR = sb("APR", [128, 7, 32], F32)
        API = sb("API", [128, 7, 32], F32)
        APR3 = sb("APR3", [128, 7, 32], F32)
        API3 = sb("API3", [128, 7, 32], F32)
        Pm = sb("Pm", [128, Q + 1, 2, 16], F32)
        sc = {nm: sb("sc_" + nm, [128, 16], F32) for nm in ("ar", "ai", "ldt", "dt", "lr", "li", "mag", "a1", "a2", "s", "c", "den", "xr", "cre", "cim", "t1", "t2")}
        dsk = sb("dsk", [128, 4], F32)
        gbT = sb("gbT", [128, 16], F32)
        tabaug = sb("tabaug", [64, 12], F32)
        Scarry = sb("Scarry", [128, 48], F32)
        rt1 = [sb(f"rt1_{i}", [128, 32], F32) for i in range(2)]
        rt2 = [sb(f"rt2_{i}", [128, 32], F32) for i in range(2)]
        rden = [sb(f"rden{i}", [128, T], F32) for i in range(2)]
        wu_sb = [sb(f"wu_sb{i}", [128, 8, 128], BF16) for i in range(2)]
        wab_sb = [sb(f"wab_sb{i}", [128, 2, 128], BF16) for i in range(2)]
        wsb_sb = [sb(f"wsb_sb{i}", [128, 4, 128], BF16) for i in range(2)]

        pall = ps("pall", [128, 4096], F32)
        pb = [pall[:, i * 512:(i + 1) * 512] for i in range(8)]

        def PB(i):
            return ("pb", i)

        def cast(out_ap, in_ap, wkey, grp):
            S.op("gpsimd", lambda e: e.dma_start(out=out_ap, in_=in_ap), writes=[wkey], dma_key=grp, final=True)

        def cast_ffn(f):
            for j in range(NJ):
                cast(Wgu[f][j], I["h_wgu_" + f][j], ("Wgu", f, j), ("su", f, j // 2))
            for j in range(NJ):
                cast(Wd[f][j * 128:(j + 1) * 128, :], I["h_wd_" + f][j * 128:(j + 1) * 128, :], ("Wd", f, j), ("suwd", f))

        if "ffn1" in stages:
            cast_ffn("ffn1")
        if "mix" in stages:
            for c in range(4):
                cast(Wu[c], I["h_wu"][c], ("Wu", c), "su_u")
            for c in range(6, 12):
                cast(Wqk[c], I["h_wqk"][c], ("Wqk", c), "su_kv")
            for g in range(3):
                cast(Wv[g], I["h_wv"][g], ("Wv", g), "su_kv")
            for c in range(6):
                cast(Wqk[c], I["h_wqk"][c], ("Wqk", c), "su_mix")
            for c in range(16):
                cast(Wg[c], I["h_wg"][c], ("Wg", c), "su_mix")
            for c in range(8):
                cast(Wglu[c], I["h_wglu"][c], ("Wglu", c), "su_mix")
                cast(Wab[c], I["h_wab"][c], ("Wab", c), "su_mix")
                cast(Wsb[c], I["h_wsb"][c], ("Wsb", c), "su_mix")
            for k in range(8):
                cast(Wo[k * 128:(k + 1) * 128, :], I["h_wo"][k * 128:(k + 1) * 128, :], ("Wo", k), "su_mix")
        if "ffn2" in stages:
            cast_ffn("ffn2")

        def ld(eng, out_ap, in_ap, wkey, dkey):
            S.op(eng, lambda e: e.dma_start(out=out_ap, in_=in_ap), writes=[wkey], dma_key=dkey)

        for n, src in (("ffn1", I["ffn1_norm"][0]), ("mix", I["mix_norm"][0]), ("ffn2", I["ffn2_norm"][0])):
            S.op("sync", lambda e, n=n, src=src: e.dma_start(out=gT[n][:], in_=src.rearrange("(k p) -> p k", p=128), allow_slow_non_contiguous=True),
                 writes=[("gT", n)], dma_key=("gT", n))
        ld("sync", idf[:], I["c_ident"], "idf", "idf")
        ld("sync", flipf[:], I["c_flip"], "flipf", "flipf")
        ld("sync", bmask[:], I["c_bmask"], "bmask", "bmask")
        S.op("sync", lambda e: e.dma_start(out=dsk[:], in_=I["ssm_d"][0].rearrange("(c p) -> p c", p=128), allow_slow_non_contiguous=True),
             writes=["dsk"], dma_key="dsk")
        S.op("sync", lambda e: e.dma_start(out=gbT[:], in_=I["gate_bias"][0].rearrange("(c p) -> p c", p=128), allow_slow_non_contiguous=True),
             writes=["gbT"], dma_key="gbT")
        S.op("vector", lambda e: e.tensor_copy(out=idb[:], in_=idf[:]), reads=["idf"], writes=["idb"])
        S.op("vector", lambda e: e.memset(ones_bf[:], 1.0), writes=["ones"])
        ld("sync", valid_f[:], I["c_valid"], "valid_f", "valid_f")
        S.op("vector", lambda e: e.tensor_copy(out=valid_bf[:], in_=valid_f[:]), reads=["valid_f"], writes=["valid"])
        for g in range(3):
            for i in range(2):
                S.op("gpsimd", lambda e, g=g, i=i: e.memset(KT[g][i][:], 0.0), writes=[("KT", g, i, s_) for s_ in range(8)])
            S.op("gpsimd", lambda e, g=g: e.memset(VR[g][:], 0.0), writes=[("V", g, i_) for i_ in range(32)])

        cnt = {}
        prefetch_x = [None]
        cur_tile = [0]

        def rot(name, n):
            v = cnt.get(name, 0)
            cnt[name] = v + 1
            return v % n

        def V(fn, reads, writes):
            S.op("vector", fn, reads=reads, writes=writes)

        def ACT(fn, reads, writes):
            S.op("scalar", fn, reads=reads, writes=writes)

        if "mix" in stages:
            S.op("sync", lambda e: e.dma_start(out=sc["ar"][:], in_=I["ssm_a_re"][0].rearrange("(p e) n -> (e n) p", e=2), allow_slow_non_contiguous=True),
                 writes=["sc_ar"], dma_key="sc_ar")
            S.op("sync", lambda e: e.dma_start(out=sc["ai"][:], in_=I["ssm_a_im"][0].rearrange("(p e) n -> (e n) p", e=2), allow_slow_non_contiguous=True),
                 writes=["sc_ai"], dma_key="sc_ai")
            for e_ in range(2):
                S.op("sync", lambda e, e_=e_: e.dma_start(out=sc["ldt"][e_ * 64:(e_ + 1) * 64, :], in_=I["ssm_log_dt"][0:1, e_:32:2].to_broadcast([64, 16]),
                                                         allow_slow_non_contiguous=True),
                     writes=[("sc_ldt", e_)], dma_key=("sc_ldt", e_))
            for nm, src in (("b_re", "ssm_b_re"), ("b_im", "ssm_b_im")):
                S.op("sync", lambda e, nm=nm, src=src: e.dma_start(out=zt[nm].v, in_=I[src][0].rearrange("(p e) n c -> (e n) p c", e=2)),
                     writes=[zt[nm].key()], dma_key=("ld", nm))
            for ri, src in enumerate(("ssm_c_re", "ssm_c_im")):
                for hs in range(2):
                    ci = ri * 2 + hs
                    for p8 in range(8):
                        for e_ in range(2):
                            gidx = 2 * (hs * 8 + p8) + e_
                            S.op("sync", lambda e, ci=ci, p8=p8, e_=e_, gidx=gidx, src=src: e.dma_start(
                                out=cnat[ci].v[p8 * 16:(p8 + 1) * 16, e_ * 64:(e_ + 1) * 64], in_=I[src][0, gidx]),
                                writes=[("cnat", ci, p8, e_)], dma_key=("cnat", ci), final=True)
            for ri, nm in enumerate(("ct_re", "ct_im")):
                for hs in range(2):
                    ci = ri * 2 + hs
                    bi = rot("pbs", 2)
                    V(lambda e, ci=ci: e.tensor_copy(out=cnatb[ci].v, in_=cnat[ci].v),
                      [cnat[ci].key()] + [("cnat", ci, a_, b_) for a_ in range(8) for b_ in range(2)], [cnatb[ci].key()])
                    S.op("tensor", lambda e, ci=ci, bi=bi: e.transpose(pb[bi].bitcast(BF16)[:, 0:128], cnatb[ci].v, idb[:]),
                         reads=[cnatb[ci].key(), "idb"], writes=[PB(bi)])
                    ACT(lambda e, nm=nm, hs=hs, bi=bi: e.copy(
                        out=zt[nm].v[:, hs * 8:(hs + 1) * 8, :], in_=pb[bi].bitcast(BF16)[:, 0:128].rearrange("q (p c) -> q p c", p=8)),
                        [PB(bi)], [zt[nm].key()])

            ACT(lambda e: e.activation(out=sc["dt"][:], in_=sc["ldt"][:], func=AF.Exp), [("sc_ldt", 0), ("sc_ldt", 1)], ["sc_dt"])
            V(lambda e: e.tensor_tensor(out=sc["lr"][:], in0=sc["ar"][:], in1=sc["dt"][:], op=ALU.mult), ["sc_ar", "sc_dt"], ["sc_lr"])
            V(lambda e: e.tensor_tensor(out=sc["li"][:], in0=sc["ai"][:], in1=sc["dt"][:], op=ALU.mult), ["sc_ai", "sc_dt"], ["sc_li"])
            MAGIC = 12582912.0
            NM = len(MV)
            ld("sync", pw["mv"].v, I["c_mvec"], pw["mv"].key(), "c_mvec")
            lrb = sc["lr"][:].unsqueeze(1).to_broadcast([128, NM, 16])
            lib = sc["li"][:].unsqueeze(1).to_broadcast([128, NM, 16])
            V(lambda e: e.tensor_tensor(out=pw["mag"].v, in0=pw["mv"].v, in1=lrb, op=ALU.mult), [pw["mv"].key(), "sc_lr"], [pw["mag"].key()])
            ACT(lambda e: e.activation(out=pw["mag"].v, in_=pw["mag"].v, func=AF.Exp), [pw["mag"].key()], [pw["mag"].key()])
            V(lambda e: e.tensor_tensor(out=pw["a1"].v, in0=pw["mv"].v, in1=lib, op=ALU.mult), [pw["mv"].key(), "sc_li"], [pw["a1"].key()])
            V(lambda e: e.tensor_scalar(out=pw["a2"].v, in0=pw["a1"].v, scalar1=0.5 * PI, scalar2=None, op0=ALU.add), [pw["a1"].key()], [pw["a2"].key()])
            for an in ("a1", "a2"):
                V(lambda e, an=an: e.tensor_scalar(out=pw["k"].v, in0=pw[an].v, scalar1=1.0 / (2 * PI), scalar2=MAGIC, op0=ALU.mult, op1=ALU.add), [pw[an].key()], [pw["k"].key()])
                V(lambda e: e.tensor_scalar(out=pw["k"].v, in0=pw["k"].v, scalar1=-MAGIC, scalar2=None, op0=ALU.add), [pw["k"].key()], [pw["k"].key()])
                V(lambda e, an=an: e.scalar_tensor_tensor(out=pw[an].v, in0=pw["k"].v, scalar=-2 * PI, in1=pw[an].v, op0=ALU.mult, op1=ALU.add), [pw["k"].key(), pw[an].key()], [pw[an].key()])
                V(lambda e, an=an: e.tensor_scalar(out=pw[an].v, in0=pw[an].v, scalar1=-PI, scalar2=PI, op0=ALU.max, op1=ALU.min), [pw[an].key()], [pw[an].key()])
                ACT(lambda e, an=an: e.activation(out=pw[an].v, in_=pw[an].v, func=AF.Sin), [pw[an].key()], [pw[an].key()])
            V(lambda e: e.tensor_tensor(out=pw["a2"].v, in0=pw["a2"].v, in1=pw["mag"].v, op=ALU.mult), [pw["a2"].key(), pw["mag"].key()], [pw["a2"].key()])
            V(lambda e: e.tensor_tensor(out=pw["a1"].v, in0=pw["a1"].v, in1=pw["mag"].v, op=ALU.mult), [pw["a1"].key(), pw["mag"].key()], [pw["a1"].key()])
            PRE, PIM = pw["a2"], pw["a1"]
            PMK = [("Pm", m_) for m_ in range(Q + 1)]
            V(lambda e: e.tensor_copy(out=Pm[:, :, 0, :], in_=PRE.v[:, 0:Q + 1, :]), [PRE.key()], PMK)
            V(lambda e: e.tensor_copy(out=Pm[:, :, 1, :], in_=PIM.v[:, 0:Q + 1, :]), [PIM.key()], PMK)
            for (TR, TI, o0, kn) in ((APR, API, Q + 1, "APW"), (APR3, API3, Q + 8, "APW3")):
                KK = [(kn, i_) for i_ in range(7)]
                V(lambda e, TR=TR, o0=o0: e.tensor_copy(out=TR[:, :, 0:16], in_=PRE.v[:, o0:o0 + 7, :]), [PRE.key()], KK)
                V(lambda e, TR=TR, o0=o0: e.tensor_copy(out=TR[:, :, 16:32], in_=PRE.v[:, o0:o0 + 7, :]), [PRE.key()], KK)
                V(lambda e, TI=TI, o0=o0: e.tensor_copy(out=TI[:, :, 16:32], in_=PIM.v[:, o0:o0 + 7, :]), [PIM.key()], KK)
                V(lambda e, TI=TI, o0=o0: e.tensor_scalar(out=TI[:, :, 0:16], in0=PIM.v[:, o0:o0 + 7, :], scalar1=-1.0, scalar2=None, op0=ALU.mult), [PIM.key()], KK)
            V(lambda e: e.tensor_tensor(out=sc["den"][:], in0=sc["ar"][:], in1=sc["ar"][:], op=ALU.mult), ["sc_ar"], ["sc_den"])
            V(lambda e: e.tensor_tensor(out=sc["t1"][:], in0=sc["ai"][:], in1=sc["ai"][:], op=ALU.mult), ["sc_ai"], ["sc_t1"])
            V(lambda e: e.tensor_tensor(out=sc["den"][:], in0=sc["den"][:], in1=sc["t1"][:], op=ALU.add), ["sc_den", "sc_t1"], ["sc_den"])
            V(lambda e: e.reciprocal(out=sc["den"][:], in_=sc["den"][:]), ["sc_den"], ["sc_den"])
            V(lambda e: e.tensor_scalar(out=sc["xr"][:], in0=Pm[:, 1, 0, :], scalar1=-1.0, scalar2=None, op0=ALU.add), [("Pm", 1)], ["sc_xr"])
            V(lambda e: e.tensor_tensor(out=sc["t1"][:], in0=sc["xr"][:], in1=sc["ar"][:], op=ALU.mult), ["sc_xr", "sc_ar"], ["sc_t1"])
            V(lambda e: e.tensor_tensor(out=sc["t2"][:], in0=Pm[:, 1, 1, :], in1=sc["ai"][:], op=ALU.mult), [("Pm", 1), "sc_ai"], ["sc_t2"])
            V(lambda e: e.tensor_tensor(out=sc["t1"][:], in0=sc["t1"][:], in1=sc["t2"][:], op=ALU.add), ["sc_t1", "sc_t2"], ["sc_t1"])
            V(lambda e: e.tensor_tensor(out=sc["cre"][:], in0=sc["t1"][:], in1=sc["den"][:], op=ALU.mult), ["sc_t1", "sc_den"], ["sc_cre"])
            V(lambda e: e.tensor_tensor(out=sc["t1"][:], in0=Pm[:, 1, 1, :], in1=sc["ar"][:], op=ALU.mult), [("Pm", 1), "sc_ar"], ["sc_t1"])
            V(lambda e: e.tensor_tensor(out=sc["t2"][:], in0=sc["xr"][:], in1=sc["ai"][:], op=ALU.mult), ["sc_xr", "sc_ai"], ["sc_t2"])
            V(lambda e: e.tensor_tensor(out=sc["t1"][:], in0=sc["t1"][:], in1=sc["t2"][:], op=ALU.subtract), ["sc_t1", "sc_t2"], ["sc_t1"])
            V(lambda e: e.tensor_tensor(out=sc["cim"][:], in0=sc["t1"][:], in1=sc["den"][:], op=ALU.mult), ["sc_t1", "sc_den"], ["sc_cim"])
            V(lambda e: e.tensor_copy(out=AR2[:, 0:16], in_=Pm[:, Q, 0, :]), [("Pm", Q)], ["AR2"])
            V(lambda e: e.tensor_copy(out=AR2[:, 16:32], in_=Pm[:, Q, 0, :]), [("Pm", Q)], ["AR2"])
            V(lambda e: e.tensor_scalar(out=AI2[:, 0:16], in0=Pm[:, Q, 1, :], scalar1=-1.0, scalar2=None, op0=ALU.mult), [("Pm", Q)], ["AI2"])
            V(lambda e: e.tensor_copy(out=AI2[:, 16:32], in_=Pm[:, Q, 1, :]), [("Pm", Q)], ["AI2"])

            def bc16(ap2):
                return ap2.unsqueeze(2).to_broadcast([128, 16, 16])

            def bc32(ap2):
                return ap2.unsqueeze(2).to_broadcast([128, 16, 32])

            for nm in ("bbz_re", "bbz_im", "ctz_re", "ctz_im", "ctzb_re", "nctzb_im"):
                V(lambda e, nm=nm: e.memset(zt[nm].v, 0.0), [], [zt[nm].key()])
            tA, tB = zt["tA"], zt["tB"]

            def interleave(dst, src_at, scale=None):
                for e_ in range(2):
                    if scale is None:
                        V(lambda e, e_=e_: e.tensor_copy(out=zt[dst].v[e_ * 64:(e_ + 1) * 64, :, e_, :], in_=src_at.v[e_ * 64:(e_ + 1) * 64, :, :]),
                          [src_at.key()], [zt[dst].key()])
                    else:
                        V(lambda e, e_=e_: e.tensor_scalar(out=zt[dst].v[e_ * 64:(e_ + 1) * 64, :, e_, :], in0=src_at.v[e_ * 64:(e_ + 1) * 64, :, :],
                                                          scalar1=scale, scalar2=None, op0=ALU.mult),
                          [src_at.key()], [zt[dst].key()])

            V(lambda e: e.tensor_tensor(out=tA.v, in0=zt["b_re"].v, in1=bc16(sc["cre"][:]), op=ALU.mult), [zt["b_re"].key(), "sc_cre"], [tA.key()])
            V(lambda e: e.tensor_tensor(out=tB.v, in0=zt["b_im"].v, in1=bc16(sc["cim"][:]), op=ALU.mult), [zt["b_im"].key(), "sc_cim"], [tB.key()])
            V(lambda e: e.tensor_tensor(out=tA.v, in0=tA.v, in1=tB.v, op=ALU.subtract), [tA.key(), tB.key()], [tA.key()])
            interleave("bbz_re", tA)
            V(lambda e: e.tensor_tensor(out=tA.v, in0=zt["b_im"].v, in1=bc16(sc["cre"][:]), op=ALU.mult), [zt["b_im"].key(), "sc_cre"], [tA.key()])
            V(lambda e: e.tensor_tensor(out=tB.v, in0=zt["b_re"].v, in1=bc16(sc["cim"][:]), op=ALU.mult), [zt["b_re"].key(), "sc_cim"], [tB.key()])
            V(lambda e: e.tensor_tensor(out=tA.v, in0=tA.v, in1=tB.v, op=ALU.add), [tA.key(), tB.key()], [tA.key()])
            interleave("bbz_im", tA)
            interleave("ctz_re", zt["ct_re"])
            interleave("ctz_im", zt["ct_im"])
            interleave("ctzb_re", zt["ct_re"])
            interleave("nctzb_im", zt["ct_im"], scale=-1.0)

            def flat(at):
                return at.v.rearrange("q p e c -> q (p e c)")

            def v3(at):
                return at.v.rearrange("q p e c -> q p (e c)")

            tmpz, tmpz2 = zt["tmpz"], zt["tmpz2"]

            def cmul(m, dre, dim, sre, sim):
                pre = Pm[:, m, 0, :]
                pim = Pm[:, m, 1, :]
                V(lambda e: e.tensor_tensor(out=v3(tmpz), in0=v3(sre), in1=bc32(pre), op=ALU.mult), [sre.key(), ("Pm", m)], [tmpz.key()])
                V(lambda e: e.tensor_tensor(out=v3(tmpz2), in0=v3(sim), in1=bc32(pim), op=ALU.mult), [sim.key(), ("Pm", m)], [tmpz2.key()])
                V(lambda e: e.tensor_tensor(out=v3(dre), in0=v3(tmpz), in1=v3(tmpz2), op=ALU.subtract), [tmpz.key(), tmpz2.key()], [dre.key()])
                V(lambda e: e.tensor_tensor(out=v3(tmpz), in0=v3(sim), in1=bc32(pre), op=ALU.mult), [sim.key(), ("Pm", m)], [tmpz.key()])
                V(lambda e: e.tensor_tensor(out=v3(tmpz2), in0=v3(sre), in1=bc32(pim), op=ALU.mult), [sre.key(), ("Pm", m)], [tmpz2.key()])
                V(lambda e: e.tensor_tensor(out=v3(dim), in0=v3(tmpz), in1=v3(tmpz2), op=ALU.add), [tmpz.key(), tmpz2.key()], [dim.key()])

            for m in range(Q + 1):
                par = m % 2
                abr, abi, car, cai = (zt[f"abz_re{par}"], zt[f"abz_im{par}"], zt[f"caz_re{par}"], zt[f"caz_im{par}"])
                if m <= Q - 1:
                    cmul(m, abr, abi, zt["bbz_re"], zt["bbz_im"])
                cmul(m, car, cai, zt["ctz_re"], zt["ctz_im"])
                if m <= Q - 1:
                    for ct in range(4):
                        bi = rot("pbs", 2)
                        S.op("tensor", lambda e, ct=ct, bi=bi, abr=abr: e.matmul(pb[bi][:, 0:128], lhsT=flat(abr)[:, ct * 128:(ct + 1) * 128],
                                                                                 rhs=flat(zt["ctzb_re"])[:, ct * 128:(ct + 1) * 128], start=True, stop=False),
                             reads=[abr.key(), zt["ctzb_re"].key()], writes=[PB(bi)])
                        S.op("tensor", lambda e, ct=ct, bi=bi, abi=abi: e.matmul(pb[bi][:, 0:128], lhsT=flat(abi)[:, ct * 128:(ct + 1) * 128],
                                                                                 rhs=flat(zt["nctzb_im"])[:, ct * 128:(ct + 1) * 128], start=False, stop=True),
                             reads=[abi.key(), zt["nctzb_im"].key()], writes=[PB(bi)])
                        if m == 0:
                            V(lambda e, bi=bi: e.tensor_tensor(out=sg[0][:, 0:128], in0=pb[bi][:, 0:128], in1=bmask[:], op=ALU.mult), [PB(bi), "bmask"], [("sg", 0)])
                            V(lambda e, ct=ct: e.scalar_tensor_tensor(out=Gt[:, ct, 0, :], in0=idf[:], scalar=dsk[:, ct:ct + 1], in1=sg[0][:, 0:128],
                                                                      op0=ALU.mult, op1=ALU.add), [("sg", 0), "idf", "dsk"], [("Gt", ct, 0)])
                        else:
                            V(lambda e, bi=bi, ct=ct, m=m: e.tensor_tensor(out=Gt[:, ct, m, :], in0=pb[bi][:, 0:128], in1=bmask[:], op=ALU.mult),
                              [PB(bi), "bmask"], [("Gt", ct, m)])
                        for ri, ab in enumerate((abr, abi)):
                            bi2 = rot("pbs", 2)
                            S.op("tensor", lambda e, ct=ct, bi2=bi2, ab=ab: e.transpose(pb[bi2].bitcast(BF16)[:, 0:128], flat(ab)[:, ct * 128:(ct + 1) * 128], idb[:]),
                                 reads=[ab.key(), "idb"], writes=[PB(bi2)])
                            ACT(lambda e, ct=ct, bi2=bi2, ri=ri, m=m: e.copy(out=Bt[:, ct, Q - 1 - m, ri, :], in_=pb[bi2].bitcast(BF16)[:, 0:128]), [PB(bi2)], [("Bt", ct)])
                if m >= 1:
                    ACT(lambda e, m=m, car=car: e.copy(out=Ct[:, :, m - 1, 0, :], in_=v3(car)), [car.key()], ["Ct"])
                    ACT(lambda e, m=m, cai=cai: e.mul(out=Ct[:, :, m - 1, 1, :], in_=v3(cai), mul=-1.0), [cai.key()], ["Ct"])

            CP("setup_ssm")
            V(lambda e: e.memset(tabaug[:], 0.0), [], ["tabaug"])
            ld("sync", tabaug[0:32, :], I["rel_bias_table"], "tabaug", "tabaug")
            V(lambda e: e.memset(tabaug[32:33, :], -30000.0), ["tabaug"], ["tabaug"])
            V(lambda e: e.memset(ohv_sb.v, 0.0), [], [ohv_sb.key()])
            S.op("sync", lambda e: e.dma_start(out=ohv_sb.v[0:33], in_=I["c_ohv"].rearrange("g c b n -> b (g c) n")), writes=[ohv_sb.key()], dma_key="ohv")
            V(lambda e: e.memset(evec_sb.v, 0.0), [], [evec_sb.key()])
            V(lambda e: e.tensor_copy(out=tabb[:], in_=tabaug[:]), ["tabaug"], ["tabb"])
            V(lambda e: e.tensor_copy(out=ohvb.v[:, :, 0:255], in_=ohv_sb.v), [ohv_sb.key()], [ohvb.key()])
            V(lambda e: e.tensor_copy(out=flipb[:], in_=flipf[:]), ["flipf"], ["flipb"])
            for g in range(3):
                bi = 2 + g
                for pc in range(2):
                    S.op("tensor", lambda e, g=g, pc=pc, bi=bi: e.matmul(pb[bi][0:4, pc * 256: pc * 256 + 255], lhsT=tabb[:, 4 * g:4 * g + 4],
                                                                         rhs=ohvb.v[:, g * 2 + pc, 0:255], start=(pc == 0), stop=True, skip_group_check=True),
                         reads=["tabb", ohvb.key()], writes=[PB(bi)])
                for pc in range(2):
                    ACT(lambda e, g=g, pc=pc, bi=bi: e.activation(out=evec_sb.v[:, g * 2 + pc, 0:255], in_=pb[bi][0:4, pc * 256: pc * 256 + 255], func=AF.Exp),
                        [PB(bi)], [evec_sb.key()])
            S.op("sync", lambda e: e.dma_start(out=evecD, in_=evec_sb.v), reads=[evec_sb.key()], writes=["evecD"], dma_key="evecD")
            for hg in range(4):
                for g in range(3):
                    for pc in range(2):
                        ri_ = rot("Rh", 4)
                        src = bass.AP(tensor=evecD.tensor, offset=(hg * 6 + g * 2 + pc) * 256, ap=[[1, 128], [1, 128]])
                        S.op("sync", lambda e, ri_=ri_, src=src: e.dma_start(out=Rh[ri_].v, in_=src), reads=["evecD"], writes=[Rh[ri_].key()], dma_key=("Rh", ri_))
                        bi = rot("pbs", 2)
                        V(lambda e, ri_=ri_: e.tensor_copy(out=Rhb[ri_].v, in_=Rh[ri_].v), [Rh[ri_].key()], [Rhb[ri_].key()])
                        S.op("tensor", lambda e, ri_=ri_, bi=bi: e.matmul(pb[bi][:, 0:128], lhsT=flipb[:], rhs=Rhb[ri_].v, start=True, stop=True),
                             reads=["flipb", Rhb[ri_].key()], writes=[PB(bi)])
                        if g < 2:
                            V(lambda e, g=g, hg=hg, pc=pc, bi=bi: e.tensor_copy(out=Ecat[g][hg][:, pc * 128:(pc + 1) * 128], in_=pb[bi][:, 0:128]),
                              [PB(bi)], [("E", g, hg)])
                        else:
                            V(lambda e, hg=hg, pc=pc, bi=bi: e.tensor_copy(
                                out=Ecat[2][hg][:].rearrange("k (q c) -> k q c", q=4)[:, :, pc * 32:(pc + 1) * 32],
                                in_=pb[bi][:, 0:128].rearrange("k (q c) -> k q c", q=4)), [PB(bi)], [("E", 2, hg)])
            S.op("gpsimd", lambda e: e.memset(Scarry[:], 0.0), writes=["Scarry"])
            CP("setup_E")

        def quarter_sq(s, qq):
            S.op("scalar", lambda e: e.activation(out=junkq[:], in_=xt[s][:, qq * 256:(qq + 1) * 256], func=AF.Square,
                                                   accum_out=ssq[:, s * 4 + qq:s * 4 + qq + 1]),
                 reads=[("xt", s)], writes=["junkq", ("ssq", s, qq)])

        SSQ_ALL = [("ssq", s_, q_) for s_ in range(NS) for q_ in range(4)]

        def norm_transpose(gname, quarters=True):
            if quarters:
                S.op("vector", lambda e: e.tensor_reduce(out=ss[:, 0:NS], in_=ssq[:].rearrange("p (s q) -> p s q", q=4), axis=mybir.AxisListType.X, op=ALU.add),
                     reads=SSQ_ALL, writes=[("ss", s_) for s_ in range(NS)])
            else:
                for s in range(NS):
                    S.op("scalar", lambda e, s=s: e.activation(out=junk[:], in_=xt[s][:], func=AF.Square, accum_out=ss[:, s:s + 1]),
                         reads=[("xt", s)], writes=["junk", ("ss", s)])
            S.op("scalar", lambda e: e.activation(out=rstd[:, 0:NS], in_=ss[:, 0:NS], func=AF.Sqrt, scale=1.0 / D, bias=EPS),
                 reads=[("ss", s_) for s_ in range(NS)], writes=[("rstd", s_) for s_ in range(NS)])
            S.op("vector", lambda e: e.reciprocal(out=rstd[:, 0:NS], in_=rstd[:, 0:NS]),
                 reads=[("rstd", s_) for s_ in range(NS)], writes=[("rstd", s_) for s_ in range(NS)])
            xbuf = [(xn[0], ("xn", 0)), (junk, "junk")]

            def emit_xn(s):
                xb, xk = xbuf[s % 2]
                S.op("scalar", lambda e: e.activation(out=xb[:], in_=xt[s][:], func=AF.Copy, scale=rstd[:, s:s + 1]),
                     reads=[("xt", s), ("rstd", s)], writes=[xk])

            emit_xn(0)
            for s in range(NS):
                if s + 1 < NS:
                    emit_xn(s + 1)
                xb, xk = xbuf[s % 2]
                pi = 6 + rot("pT", 2)
                pTv = pb[pi].bitcast(BF16)
                for k in range(8):
                    S.op("tensor", lambda e, k=k, xb=xb, pTv=pTv: e.transpose(pTv[:, k * 128:(k + 1) * 128], xb[:, k * 128:(k + 1) * 128], idb[:]),
                         reads=[xk, "idb"], writes=[PB(pi)])
                S.op("vector", lambda e, s=s, pTv=pTv: e.tensor_tensor(out=xT[:, :, s * 128:(s + 1) * 128], in0=pTv.rearrange("p (k c) -> p k c", k=8),
                                                                      in1=gT[gname][:].unsqueeze(2).to_broadcast([128, 8, 128]), op=ALU.mult),
                     reads=[PB(pi), ("gT", gname)], writes=[("xT", s)])

        XT_ALL = [("xT", s) for s in range(NS)]

        def ffn(f):
            norm_transpose(f, quarters=(f != "ffn1"))
            wdv = Wd[f].rearrange("(j p) n -> p j n", p=128)

            def load_wd(qq):
                si = qq % 2
                S.op("sync", lambda e, qq=qq, si=si: e.dma_start(out=wd_sb[si].v, in_=wdv[:, :, qq * 256:(qq + 1) * 256]),
                     reads=[("Wd", f, j_) for j_ in range(NJ)], writes=[wd_sb[si].key()], dma_key=("wd_sb", si))

            def load_wgu(j):
                si = j % 3
                S.op("sync", lambda e, j=j, si=si: e.dma_start(out=wgu_sb[si].v, in_=Wgu[f][j]),
                     reads=[("Wgu", f, j)], writes=[wgu_sb[si].key()], dma_key=("wgu_sb", si))

            load_wgu(0)
            load_wgu(1)
            load_wd(0)
            load_wd(1)
            for j in range(NJ):
                if j + 2 < NJ:
                    load_wgu(j + 2)
                si = j % 3
                gi = rot("pG", 2)
                pG, pU = pb[gi], pb[2 + gi]
                for k in range(8):
                    S.op("tensor", lambda e, k=k, si=si, pG=pG: e.matmul(pG, lhsT=wgu_sb[si].v[:, k, 0:128], rhs=xT[:, k, :], start=(k == 0), stop=(k == 7)),
                         reads=[wgu_sb[si].key()] + XT_ALL, writes=[PB(gi)])
                for k in range(8):
                    S.op("tensor", lambda e, k=k, si=si, pU=pU: e.matmul(pU, lhsT=wgu_sb[si].v[:, k, 128:256], rhs=xT[:, k, :], start=(k == 0), stop=(k == 7)),
                         reads=[wgu_sb[si].key()] + XT_ALL, writes=[PB(2 + gi)])
                gg = rot("sg", 2)
                S.op("scalar", lambda e, gg=gg, pG=pG: e.activation(out=sg[gg][:], in_=pG, func=AF.Silu),
                     reads=[PB(gi)], writes=[("sg", gg)])
                S.op("vector", lambda e, gg=gg, pU=pU, j=j: e.tensor_tensor(out=hT.v[:, j, :], in0=sg[gg][:], in1=pU, op=ALU.mult),
                     reads=[("sg", gg), PB(2 + gi)], writes=[hT.sub(j, NJ)])
            for qq in range(4):
                si = qq % 2
                for s in range(NS):
                    di = 4 + rot("pD", 2)
                    pD = pb[di][:, 0:256]
                    for j in range(NJ):
                        S.op("tensor", lambda e, j=j, s=s, si=si, pD=pD: e.matmul(pD, lhsT=hT.v[:, j, s * 128:(s + 1) * 128], rhs=wd_sb[si].v[:, j, :],
                                                                                   start=(j == 0), stop=(j == NJ - 1)),
                             reads=[hT.sub(j, NJ), wd_sb[si].key()], writes=[PB(di)])
                    S.op("vector", lambda e, s=s, qq=qq, pD=pD: e.scalar_tensor_tensor(
                        out=xt[s][:, qq * 256:(qq + 1) * 256], in0=pD, scalar=0.5, in1=xt[s][:, qq * 256:(qq + 1) * 256], op0=ALU.mult, op1=ALU.add),
                        reads=[PB(di), ("xt", s)], writes=[("xt", s)])
                    quarter_sq(s, qq)
                if qq + 2 < 4:
                    load_wd(qq + 2)

        def MM(out_ap, lhsT, rhs, start, reads, writes, tp=None, sgc=False):
            kw = {}
            if tp is not None:
                kw["tile_position"] = tp
            kw["skip_group_check"] = True
            S.op("tensor", lambda e: e.matmul(out_ap, lhsT=lhsT, rhs=rhs, start=start, stop=False, **kw), reads=reads, writes=writes)

        def mixer(t, light=False):
            norm_transpose("mix")
            if light and S.enabled:
                for s_ in range(NS):
                    prefetch_x[0](t + 1, s_)
            nb3 = t // 4
            q3 = t % 4
            CP("normT")
            for c in range(4):
                wi = rot("wu", 2)
                S.op("sync", lambda e, c=c, wi=wi: e.dma_start(out=wu_sb[wi][:], in_=Wu[c]), reads=[("Wu", c)], writes=[("wu_sb", wi)], dma_key=("wu_sb", wi))
                bi = (0, 1, 6, 7)[rot("pbA", 4)]
                for k in range(8):
                    MM(pb[bi], wu_sb[wi][:, k, :], xT[:, k, :], k == 0, [("wu_sb", wi)] + XT_ALL, [PB(bi)])
                S.op("scalar", lambda e, c=c, bi=bi: e.copy(out=uT.v[:, c, :], in_=pb[bi]), reads=[PB(bi)], writes=[uT.sub(c, 4)])
            CP("u_proj")
            firstL = [True] * 4
            for ri in range(2):
                for ct in range(4):
                    a_ = ri * 4 + ct
                    for j in range(Q):
                        for pp in range(4):
                            bi = 2 + pp
                            MM(pb[bi][:, a_ * NCH:(a_ + 1) * NCH], Bt[32 * pp:32 * pp + 32, ct, j, ri, :], uT.v[32 * pp:32 * pp + 32, ct, j::Q], firstL[pp],
                               [("Bt", ct), uT.sub(ct, 4)], [PB(bi)], tp=(32 * pp, 0), sgc=True)
                            firstL[pp] = False
            for pp in range(4):
                bi = 2 + pp
                S.op("vector", lambda e, pp=pp, bi=bi: e.tensor_copy(
                    out=Hs.v[:, 1:NCH + 1, pp:32:4], in_=pb[bi].rearrange("q (a k) -> q k a", a=8)),
                    reads=[PB(bi)], writes=[Hs.sub(k_, NCH + 1) for k_ in range(1, NCH + 1)])
            CP("ssm_L")
            if light:
                tt1 = AT(QT.off, [128, 32, 32], F32)
                tt2 = AT(QT.off + 4096, [128, 32, 32], F32)
                HS_L = [Hs.sub(k_, NCH + 1) for k_ in range(1, NCH + 1)]
                S.op("gpsimd", lambda e: e.tensor_copy(out=Hs.v[:, 1:NCH + 1, 32:48], in_=Hs.v[:, 1:NCH + 1, 0:16]), reads=HS_L, writes=HS_L)
                for l in range(1, 7):
                    step, half = 2 ** l, 2 ** (l - 1)
                    n = NCH // step
                    left = Hs.v[:, half:NCH + 1:step, :]
                    right = Hs.v[:, step:NCH + 1:step, :]
                    arb = APR[:, l - 1, :].unsqueeze(1).to_broadcast([128, n, 32])
                    aib = API[:, l - 1, :].unsqueeze(1).to_broadcast([128, n, 32])
                    S.op("gpsimd", lambda e, left=left, arb=arb, n=n: e.tensor_tensor(out=tt1.v[:, 0:n, :], in0=left[:, :, 0:32], in1=arb, op=ALU.mult),
                         reads=HS_L + [("APW", l - 1)], writes=[tt1.key()])
                    S.op("gpsimd", lambda e, left=left, aib=aib, n=n: e.tensor_tensor(out=tt2.v[:, 0:n, :], in0=left[:, :, 16:48], in1=aib, op=ALU.mult),
                         reads=HS_L + [("APW", l - 1)], writes=[tt2.key()])
                    S.op("gpsimd", lambda e, n=n: e.tensor_tensor(out=tt1.v[:, 0:n, :], in0=tt1.v[:, 0:n, :], in1=tt2.v[:, 0:n, :], op=ALU.add),
                         reads=[tt1.key(), tt2.key()], writes=[tt1.key()])
                    S.op("gpsimd", lambda e, right=right, n=n: e.tensor_tensor(out=right[:, :, 0:32], in0=right[:, :, 0:32], in1=tt1.v[:, 0:n, :], op=ALU.add),
                         reads=HS_L + [tt1.key()], writes=HS_L)
                    S.op("gpsimd", lambda e, right=right: e.tensor_copy(out=right[:, :, 32:48], in_=right[:, :, 0:16]), reads=HS_L, writes=HS_L)
                a_ = rot("rt", 2)
                S.op("gpsimd", lambda e, a_=a_: e.tensor_tensor(out=rt1[a_][:], in0=Scarry[:, 0:32], in1=APR[:, 6, :], op=ALU.mult), reads=["Scarry", ("APW", 6)], writes=[("rt1", a_)])
                S.op("gpsimd", lambda e, a_=a_: e.tensor_tensor(out=rt2[a_][:], in0=Scarry[:, 16:48], in1=API[:, 6, :], op=ALU.mult), reads=["Scarry", ("APW", 6)], writes=[("rt2", a_)])
                S.op("gpsimd", lambda e, a_=a_: e.tensor_tensor(out=rt1[a_][:], in0=rt1[a_][:], in1=rt2[a_][:], op=ALU.add), reads=[("rt1", a_), ("rt2", a_)], writes=[("rt1", a_)])
                S.op("gpsimd", lambda e, a_=a_: e.tensor_tensor(out=Scarry[:, 0:32], in0=rt1[a_][:], in1=Hs.v[:, NCH, 0:32], op=ALU.add), reads=[("rt1", a_)] + HS_L, writes=["Scarry"])
                S.op("gpsimd", lambda e: e.tensor_copy(out=Scarry[:, 32:48], in_=Scarry[:, 0:16]), reads=["Scarry"], writes=["Scarry"])
            else:
                HS_L = [Hs.sub(k_, NCH + 1) for k_ in range(1, NCH + 1)]
                HS_A = [Hs.sub(k_, NCH + 1) for k_ in range(0, NCH + 1)]
                G = lambda fn, reads, writes: S.op("gpsimd", fn, reads=reads, writes=writes)
                G(lambda e: e.tensor_copy(out=Hs.v[:, 0, :], in_=Scarry[:]), ["Scarry"], [Hs.sub(0, NCH + 1)])
                G(lambda e: e.tensor_copy(out=Hs.v[:, 1:NCH + 1, 32:48], in_=Hs.v[:, 1:NCH + 1, 0:16]), HS_L, HS_L)
                ar8 = AR2[:].unsqueeze(1).to_broadcast([128, 8, 32])
                ai8 = AI2[:].unsqueeze(1).to_broadcast([128, 8, 32])
                for r in range(1, 8):
                    cur = Hs.v[:, r + 1:NCH + 1:8, :]
                    prev = Hs.v[:, r:NCH:8, :]
                    G(lambda e, prev=prev: e.tensor_tensor(out=bs1.v, in0=prev[:, :, 0:32], in1=ar8, op=ALU.mult), HS_L + ["AR2"], [bs1.key()])
                    G(lambda e, prev=prev: e.tensor_tensor(out=bs2.v, in0=prev[:, :, 16:48], in1=ai8, op=ALU.mult), HS_L + ["AI2"], [bs2.key()])
                    G(lambda e: e.tensor_tensor(out=bs1.v, in0=bs1.v, in1=bs2.v, op=ALU.add), [bs1.key(), bs2.key()], [bs1.key()])
                    G(lambda e, cur=cur: e.tensor_tensor(out=cur[:, :, 0:32], in0=cur[:, :, 0:32], in1=bs1.v, op=ALU.add), HS_L + [bs1.key()], HS_L)
                    G(lambda e, cur=cur: e.tensor_copy(out=cur[:, :, 32:48], in_=cur[:, :, 0:16]), HS_L, HS_L)
                for b in range(8):
                    a = rot("rt", 2)
                    sp_, sc_ = 8 * b, 8 * b + 8
                    G(lambda e, a=a, sp_=sp_: e.tensor_tensor(out=rt1[a][:], in0=Hs.v[:, sp_, 0:32], in1=APR[:, 3, :], op=ALU.mult), HS_A + [("APW", 3)], [("rt1", a)])
                    G(lambda e, a=a, sp_=sp_: e.tensor_tensor(out=rt2[a][:], in0=Hs.v[:, sp_, 16:48], in1=API[:, 3, :], op=ALU.mult), HS_A + [("APW", 3)], [("rt2", a)])
                    G(lambda e, a=a: e.tensor_tensor(out=rt1[a][:], in0=rt1[a][:], in1=rt2[a][:], op=ALU.add), [("rt1", a), ("rt2", a)], [("rt1", a)])
                    G(lambda e, a=a, sc_=sc_: e.tensor_tensor(out=Hs.v[:, sc_, 0:32], in0=Hs.v[:, sc_, 0:32], in1=rt1[a][:], op=ALU.add), HS_A + [("rt1", a)], HS_A)
                    G(lambda e, sc_=sc_: e.tensor_copy(out=Hs.v[:, sc_, 32:48], in_=Hs.v[:, sc_, 0:16]), HS_A, HS_A)
                tprev = Hs.v[:, 0:NCH:8, :]
                for r in range(7):
                    cur = Hs.v[:, r + 1:NCH + 1:8, :]
                    arr = APR3[:, r, :].unsqueeze(1).to_broadcast([128, 8, 32])
                    aii = API3[:, r, :].unsqueeze(1).to_broadcast([128, 8, 32])
                    G(lambda e, arr=arr: e.tensor_tensor(out=bs1.v, in0=tprev[:, :, 0:32], in1=arr, op=ALU.mult), HS_A + [("APW3", r)], [bs1.key()])
                    G(lambda e, aii=aii: e.tensor_tensor(out=bs2.v, in0=tprev[:, :, 16:48], in1=aii, op=ALU.mult), HS_A + [("APW3", r)], [bs2.key()])
                    G(lambda e: e.tensor_tensor(out=bs1.v, in0=bs1.v, in1=bs2.v, op=ALU.add), [bs1.key(), bs2.key()], [bs1.key()])
                    G(lambda e, cur=cur: e.tensor_tensor(out=cur[:, :, 0:32], in0=cur[:, :, 0:32], in1=bs1.v, op=ALU.add), HS_A + [bs1.key()], HS_A)
            CP("recur")
            if light:
                need_g = set()
                if t >= n_prefix - 4:
                    need_g.add(2)
                if t == n_prefix - 1:
                    need_g.update((0, 1))
            else:
                need_g = {0, 1, 2}
            for c in range(12):
                if c < 6 and light:
                    continue
                if c >= 6 and (c - 6) // 2 not in need_g:
                    continue
                wi = rot("wqk", 2)
                S.op("sync", lambda e, c=c, wi=wi: e.dma_start(out=wqk_sb[wi].v, in_=Wqk[c]), reads=[("Wqk", c)], writes=[wqk_sb[wi].key()], dma_key=("wqk_sb", wi))
                bi = (0, 1, 6, 7)[rot("pbA", 4)]
                for k in range(8):
                    MM(pb[bi], wqk_sb[wi].v[:, k, :], xT[:, k, :], k == 0, [wqk_sb[wi].key()] + XT_ALL, [PB(bi)])
                if c < 6:
                    S.op("scalar", lambda e, c=c, bi=bi: e.mul(out=QT.v[:, c, :], in_=pb[bi], mul=0.125), reads=[PB(bi)], writes=[QT.sub(c, 6)])
                else:
                    g, sp = (c - 6) // 2, (c - 6) % 2
                    if g < 2:
                        off, slot = (t % 2) * T, t % 2
                    else:
                        off, slot = (nb3 % 2) * 2048 + q3 * T, (nb3 % 2) * 4 + q3
                    S.op("vector", lambda e, g=g, sp=sp, off=off, bi=bi: e.tensor_copy(out=KT[g][sp][:, off:off + T], in_=pb[bi]),
                         reads=[PB(bi)], writes=[("KT", g, sp, slot)])
            CP("qk")
            for g in range(3):
                if g not in need_g:
                    continue
                wi = rot("wv", 2)
                S.op("sync", lambda e, g=g, wi=wi: e.dma_start(out=wv_sb[wi].v, in_=Wv[g]), reads=[("Wv", g)], writes=[wv_sb[wi].key()], dma_key=("wv_sb", wi))
                if g == 0:
                    for b in range(4):
                        bi = 6 + rot("pbV", 2)
                        for k in range(8):
                            MM(pb[bi][:, 0:256], xT[:, k, b * 128:(b + 1) * 128], wv_sb[wi].v[:, k, :], k == 0, [wv_sb[wi].key()] + XT_ALL, [PB(bi)])
                        vi = (4 * t + b) % 8
                        S.op("scalar", lambda e, bi=bi, vi=vi: e.copy(out=VR[0][:, vi, :], in_=pb[bi][:, 0:256]), reads=[PB(bi)], writes=[("V", 0, vi)])
                elif g == 1:
                    for r in range(4):
                        bi = 6 + rot("pbV", 2)
                        for k in range(8):
                            MM(pb[bi][:, 0:256], xT[:, k, r::4], wv_sb[wi].v[:, k, :], k == 0, [wv_sb[wi].key()] + XT_ALL, [PB(bi)])
                        vi = (t % 2) * 4 + r
                        S.op("scalar", lambda e, bi=bi, vi=vi: e.copy(out=VR[1][:, vi, :], in_=pb[bi][:, 0:256]), reads=[PB(bi)], writes=[("V", 1, vi)])
                else:
                    for r2 in range(8):
                        bi = 6 + rot("pbV", 2)
                        for rr in range(2):
                            r = 2 * r2 + rr
                            for k in range(8):
                                MM(pb[bi][32 * q3:32 * q3 + 32, rr * 256:(rr + 1) * 256], xT[:, k, r::16], wv_sb[wi].v[:, k, :], (k == 0 and rr == 0),
                                   [wv_sb[wi].key()] + XT_ALL, [PB(bi)], tp=(0, 32 * q3), sgc=True)
                        vi = (nb3 % 2) * 16 + 2 * r2
                        S.op("scalar", lambda e, bi=bi, vi=vi: e.copy(out=VR[2][32 * q3:32 * q3 + 32, vi:vi + 2, :],
                                                                     in_=pb[bi][32 * q3:32 * q3 + 32, :].rearrange("k (r c) -> k r c", r=2)),
                             reads=[PB(bi)], writes=[("V", 2, vi), ("V", 2, vi + 1)])
            CP("v")
            if light:
                return
            started = set()

            def acc(bank, half, cols_ap, lhsT, rhs, reads):
                first = (bank, half) not in started
                started.add((bank, half))
                MM(cols_ap, lhsT, rhs, first, reads, [PB(bank)], tp=(0, 64 * half), sgc=True)

            groups = []
            for g in range(3):
                for hg in range(4):
                    sp, hh = hg // 2, hg % 2
                    qv = QT.v[64 * hh:64 * hh + 64, 2 * g + sp, :]
                    kt = KT[g][sp]
                    units = []
                    if g == 0:
                        for b in range(4):
                            qs = slice(b * 128, (b + 1) * 128)
                            blk = 4 * t + b
                            for pc in range(2):
                                kb = blk - 1 + pc
                                if kb < 0:
                                    units.append(None)
                                    continue
                                ko = (kb * 128) % (2 * T)
                                units.append((qs, kt[64 * hh:64 * hh + 64, ko:ko + 128], ("V", 0, kb % 8), VR[0][:, kb % 8, hg * 64:(hg + 1) * 64],
                                              [("KT", 0, sp, (kb // 4) % 2)], kb // 4))
                        W = 128
                    elif g == 1:
                        for r in range(4):
                            qs = slice(r, T, 4)
                            for pc in range(2):
                                tt = t - 1 + pc
                                if tt < 0:
                                    units.append(None)
                                    continue
                                ko = (tt % 2) * T
                                vi = (tt % 2) * 4 + r
                                units.append((qs, kt[64 * hh:64 * hh + 64, ko + r:ko + T:4], ("V", 1, vi), VR[1][:, vi, hg * 64:(hg + 1) * 64], [("KT", 1, sp, tt % 2)], tt))
                        W = 128
                    else:
                        for r in range(16):
                            qs = slice(r, T, 16)
                            for pc in range(2):
                                nb = nb3 - 1 + pc
                                if nb < 0:
                                    units.append(None)
                                    continue
                                ko = (nb % 2) * 2048
                                vi = (nb % 2) * 16 + r
                                units.append((qs, kt[64 * hh:64 * hh + 64, ko + r:ko + 2048:16], ("V", 2, vi), VR[2][:, vi, hg * 64:(hg + 1) * 64],
                                              [("KT", 2, sp, (nb % 2) * 4 + s_) for s_ in range(4)], nb * 4))
                        W = 32
                    per_bank = T // W
                    for u0 in range(0, len(units), per_bank):
                        grp = units[u0:u0 + per_bank]
                        if all(u is None for u in grp):
                            continue
                        groups.append((g, hg, sp, hh, qv, W, grp))

            SB = (0, 1, 6, 7)

            def emit_S(i):
                g, hg, sp, hh, qv, W, grp = groups[i]
                bi = SB[i % 4]
                for ui, u in enumerate(grp):
                    if u is None:
                        continue
                    qs, kap, vkey, vap, kkeys, ktile = u
                    MM(pb[bi][:, ui * W:(ui + 1) * W], kap, qv[:, qs], True, kkeys + [QT.sub(2 * g + sp, 6)], [PB(bi)], tp=(64 * hh, 0), sgc=True)

            def emit_E(i):
                g, hg, sp, hh, qv, W, grp = groups[i]
                bi, ei, pi_ = SB[i % 4], i % 2, i % 2
                S.op("scalar", lambda e: e.activation(out=sg[ei][:], in_=pb[bi], func=AF.Exp), reads=[PB(bi)], writes=[("sg", ei)])
                if g < 2:
                    ein = Ecat[g][hg][:].unsqueeze(1).to_broadcast([128, 2, 256])
                    S.op("vector", lambda e: e.tensor_tensor(
                        out=PT[pi_].v.rearrange("k (a c) -> k a c", a=2), in0=sg[ei][:].rearrange("k (a c) -> k a c", a=2), in1=ein, op=ALU.mult),
                        reads=[("sg", ei), ("E", g, hg)], writes=[PT[pi_].key()])
                else:
                    ein = Ecat[2][hg][:, q3 * 64:(q3 + 1) * 64].unsqueeze(1).to_broadcast([128, 8, 64])
                    S.op("vector", lambda e: e.tensor_tensor(
                        out=PT[pi_].v.rearrange("k (a c) -> k a c", a=8), in0=sg[ei][:].rearrange("k (a c) -> k a c", a=8), in1=ein, op=ALU.mult),
                        reads=[("sg", ei), ("E", 2, hg)], writes=[PT[pi_].key()])

            def emit_PV(i):
                g, hg, sp, hh, qv, W, grp = groups[i]
                pi_ = i % 2
                for ui, u in enumerate(grp):
                    if u is None:
                        continue
                    qs, kap, vkey, vap, kkeys, ktile = u
                    prhs = PT[pi_].v[:, ui * W:(ui + 1) * W]
                    acc(2 + sp, hh, pb[2 + sp][64 * hh:64 * hh + 64, qs], vap, prhs, [vkey, PT[pi_].key()])
                    if ktile < n_prefix:
                        acc(4 + sp, hh, pb[4 + sp][64 * hh:64 * hh + 64, qs], valid_bf[:], prhs, ["valid", PT[pi_].key()])
                    else:
                        acc(4 + sp, hh, pb[4 + sp][64 * hh:64 * hh + 64, qs], ones_bf[:], prhs, ["ones", PT[pi_].key()])

            ng = len(groups)
            for i in range(min(2, ng)):
                emit_S(i)
            emit_E(0)
            for i in range(ng):
                if i + 2 < ng:
                    emit_S(i + 2)
                if i + 1 < ng:
                    emit_E(i + 1)
                emit_PV(i)
            for sp in range(2):
                S.op("vector", lambda e, sp=sp: e.reciprocal(out=rden[sp][:], in_=pb[4 + sp]), reads=[PB(4 + sp)], writes=[("rden", sp)])
                S.op("vector", lambda e, sp=sp: e.tensor_tensor(out=oT.v[:, sp, :], in0=pb[2 + sp], in1=rden[sp][:], op=ALU.mult),
                     reads=[PB(2 + sp), ("rden", sp)], writes=[oT.sub(sp, 2)])
            CP("attn")
            HS_ALL = [Hs.sub(k_, NCH + 1) for k_ in range(NCH)]
            for ri in range(2):
                S.op("scalar", lambda e, ri=ri: e.copy(out=Sb[ri].v, in_=Hs.v[:, 0:NCH, ri * 16:(ri + 1) * 16].rearrange("q k p -> q p k")),
                     reads=HS_ALL, writes=[Sb[ri].key()])
            S.op("gpsimd", lambda e: e.tensor_copy(out=Scarry[:], in_=Hs.v[:, NCH, :]), reads=[Hs.sub(NCH, NCH + 1)], writes=["Scarry"])
            YB = (6, 7, 0, 1)
            for ct in range(4):
                bi = YB[ct]
                first = True
                for tau in range(Q):
                    for l in range(tau + 1):
                        MM(pb[bi][:, tau::Q], Gt[:, ct, l, :], uT.v[:, ct, (tau - l)::Q], first, [("Gt", ct, l), uT.sub(ct, 4)], [PB(bi)], sgc=True)
                        first = False
            for ct in range(4):
                bi = YB[ct]
                for j in range(Q):
                    for ri in range(2):
                        for pp in range(4):
                            p = 4 * ct + pp
                            MM(pb[bi][32 * pp:32 * pp + 32, j::Q], Ct[:, p, j, ri, :], Sb[ri].v[:, p, :], False, ["Ct", Sb[ri].key()], [PB(bi)], tp=(0, 32 * pp), sgc=True)
                yi = rot("ysb", 2)
                S.op("scalar", lambda e, bi=bi, yi=yi: e.copy(out=ysb[yi].v, in_=pb[bi]), reads=[PB(bi)], writes=[ysb[yi].key()])
                S.op("vector", lambda e, yi=yi: e.tensor_tensor(out=rden[yi][:], in0=ysb[yi].v, in1=ysb[yi].v, op=ALU.mult), reads=[ysb[yi].key()], writes=[("rden", yi)])
                S.op("vector", lambda e, yi=yi: e.tensor_scalar(out=rden[yi][:], in0=rden[yi][:], scalar1=0.044715, scalar2=1.0, op0=ALU.mult, op1=ALU.add),
                     reads=[("rden", yi)], writes=[("rden", yi)])
                S.op("vector", lambda e, yi=yi: e.tensor_tensor(out=rden[yi][:], in0=rden[yi][:], in1=ysb[yi].v, op=ALU.mult), reads=[("rden", yi), ysb[yi].key()], writes=[("rden", yi)])
                S.op("scalar", lambda e, yi=yi: e.activation(out=rden[yi][:], in_=rden[yi][:], func=AF.Sigmoid, scale=2.0 * math.sqrt(2.0 / math.pi)),
                     reads=[("rden", yi)], writes=[("rden", yi)])
                S.op("vector", lambda e, yi=yi, ct=ct: e.tensor_tensor(out=geT.v[:, ct, :], in0=rden[yi][:], in1=ysb[yi].v, op=ALU.mult),
                     reads=[("rden", yi), ysb[yi].key()], writes=[geT.sub(ct, 4)])
            CP("ssm_y")
            GE_ALL = [geT.sub(c_, 4) for c_ in range(4)]
            for f in range(4):
                wa, wb = rot("wglu", 4), rot("wglu", 4)
                S.op("sync", lambda e, f=f, wa=wa: e.dma_start(out=wglu_sb[wa].v, in_=Wglu[f]), reads=[("Wglu", f)], writes=[wglu_sb[wa].key()], dma_key=("wglu_sb", wa))
                S.op("sync", lambda e, f=f, wb=wb: e.dma_start(out=wglu_sb[wb].v, in_=Wglu[4 + f]), reads=[("Wglu", 4 + f)], writes=[wglu_sb[wb].key()], dma_key=("wglu_sb", wb))
                ba, bb_ = (4, 5) if f % 2 == 0 else (2, 3)
                for k in range(4):
                    MM(pb[ba], wglu_sb[wa].v[:, k, :], geT.v[:, k, :], k == 0, [wglu_sb[wa].key(), geT.sub(k, 4)], [PB(ba)])
                for k in range(4):
                    MM(pb[bb_], wglu_sb[wb].v[:, k, :], geT.v[:, k, :], k == 0, [wglu_sb[wb].key(), geT.sub(k, 4)], [PB(bb_)])
                gi_ = rot("sgE", 2)
                S.op("scalar", lambda e, gi_=gi_, bb_=bb_: e.activation(out=sg[gi_][:], in_=pb[bb_], func=AF.Sigmoid), reads=[PB(bb_)], writes=[("sg", gi_)])
                S.op("vector", lambda e, gi_=gi_, ba=ba, f=f: e.tensor_tensor(out=ys2T.v[:, f, :], in0=sg[gi_][:], in1=pb[ba], op=ALU.mult),
                     reads=[("sg", gi_), PB(ba)], writes=[ys2T.sub(f, 4)])
            CP("glu")
            YS_ALL = [ys2T.sub(f_, 4) for f_ in range(4)]
            for dt_ in range(8):
                w1, w2 = rot("wab", 2), rot("wsb", 2)
                w3, w4 = rot("wg", 4), rot("wg", 4)
                S.op("sync", lambda e, dt_=dt_, w1=w1: e.dma_start(out=wab_sb[w1][:], in_=Wab[dt_]), reads=[("Wab", dt_)], writes=[("wab_sb", w1)], dma_key=("wab_sb", w1))
                S.op("sync", lambda e, dt_=dt_, w2=w2: e.dma_start(out=wsb_sb[w2][:], in_=Wsb[dt_]), reads=[("Wsb", dt_)], writes=[("wsb_sb", w2)], dma_key=("wsb_sb", w2))
                S.op("sync", lambda e, dt_=dt_, w3=w3: e.dma_start(out=wg_sb[w3].v, in_=Wg[dt_]), reads=[("Wg", dt_)], writes=[wg_sb[w3].key()], dma_key=("wg_sb", w3))
                S.op("sync", lambda e, dt_=dt_, w4=w4: e.dma_start(out=wg_sb[w4].v, in_=Wg[8 + dt_]), reads=[("Wg", 8 + dt_)], writes=[wg_sb[w4].key()], dma_key=("wg_sb", w4))
                b0 = 0 if dt_ % 2 == 0 else 4
                for k in range(2):
                    MM(pb[b0], wab_sb[w1][:, k, :], oT.v[:, k, :], k == 0, [("wab_sb", w1), oT.sub(0, 2), oT.sub(1, 2)], [PB(b0)])
                for k in range(4):
                    MM(pb[b0 + 1], wsb_sb[w2][:, k, :], ys2T.v[:, k, :], k == 0, [("wsb_sb", w2)] + YS_ALL, [PB(b0 + 1)])
                for k in range(8):
                    MM(pb[b0 + 2], wg_sb[w3].v[:, k, :], xT[:, k, :], k == 0, [wg_sb[w3].key()] + XT_ALL, [PB(b0 + 2)])
                for k in range(8):
                    MM(pb[b0 + 3], wg_sb[w4].v[:, k, :], xT[:, k, :], k == 0, [wg_sb[w4].key()] + XT_ALL, [PB(b0 + 3)])
                ya, ys_ = rot("ysb", 2), rot("ysb", 2)
                S.op("scalar", lambda e, dt_=dt_, b0=b0, ya=ya: e.activation(out=ysb[ya].v, in_=pb[b0 + 2], func=AF.Sigmoid, bias=gbT[:, dt_:dt_ + 1]),
                     reads=[PB(b0 + 2), "gbT"], writes=[ysb[ya].key()])
                S.op("scalar", lambda e, dt_=dt_, b0=b0, ys_=ys_: e.activation(out=ysb[ys_].v, in_=pb[b0 + 3], func=AF.Sigmoid, bias=gbT[:, 8 + dt_:9 + dt_]),
                     reads=[PB(b0 + 3), "gbT"], writes=[ysb[ys_].key()])
                S.op("vector", lambda e, b0=b0, ya=ya: e.tensor_tensor(out=ysb[ya].v, in0=ysb[ya].v, in1=pb[b0], op=ALU.mult), reads=[ysb[ya].key(), PB(b0)], writes=[ysb[ya].key()])
                S.op("vector", lambda e, b0=b0, ys_=ys_: e.tensor_tensor(out=ysb[ys_].v, in0=ysb[ys_].v, in1=pb[b0 + 1], op=ALU.mult), reads=[ysb[ys_].key(), PB(b0 + 1)], writes=[ysb[ys_].key()])
                S.op("vector", lambda e, dt_=dt_, ya=ya, ys_=ys_: e.tensor_tensor(out=mT.v[:, dt_, :], in0=ysb[ya].v, in1=ysb[ys_].v, op=ALU.add),
                     reads=[ysb[ya].key(), ysb[ys_].key()], writes=[mT.sub(dt_, 8)])
            CP("merge")
            MT_ALL = [mT.sub(k_, 8) for k_ in range(8)]
            wov = Wo.rearrange("(k p) n -> p k n", p=128)
            for qq in range(4):
                wi = rot("wo", 2)
                S.op("sync", lambda e, qq=qq, wi=wi: e.dma_start(out=wo_sb[wi].v, in_=wov[:, :, qq * 256:(qq + 1) * 256]),
                     reads=[("Wo", k_) for k_ in range(8)], writes=[wo_sb[wi].key()], dma_key=("wo_sb", wi))
                for s in range(NS):
                    bi = rot("pbO", 4)
                    for k in range(8):
                        MM(pb[bi][:, 0:256], mT.v[:, k, s * 128:(s + 1) * 128], wo_sb[wi].v[:, k, :], k == 0, MT_ALL + [wo_sb[wi].key()], [PB(bi)])
                    S.op("vector", lambda e, s=s, qq=qq, bi=bi: e.tensor_tensor(out=xt[s][:, qq * 256:(qq + 1) * 256], in0=pb[bi][:, 0:256],
                                                                                in1=xt[s][:, qq * 256:(qq + 1) * 256], op=ALU.add),
                         reads=[PB(bi), ("xt", s)], writes=[("xt", s)])
                    quarter_sq(s, qq)

        x_loaded = set()

        def load_x(t, s):
            if t >= n_tiles or (t, s) in x_loaded:
                return
            x_loaded.add((t, s))
            r0 = t * T + s * 128
            S.op("sync", lambda e: e.dma_start(out=xt[s][:], in_=I["x"][r0:r0 + 128, :]), writes=[("xt", s)], dma_key=("xl", s))

        prefetch_x[0] = load_x
        for t in range(n_tiles):
            light = t < n_prefix
            cur_tile[0] = t
            for s in range(NS):
                load_x(t, s)
            if "ffn1" in stages:
                ffn("ffn1")
            if "mix" in stages:
                mixer(t, light)
            if light:
                S.enabled = True
                continue
            if "ffn2" in stages:
                ffn("ffn2")
            S.enabled = True
            S.op("sync", lambda e: e.dma_start(out=grow_final.v, in_=I["final_norm"].unsqueeze(0).to_broadcast([128, D])),
                 writes=[grow_final.key()], dma_key="grow_final")
            S.op("vector", lambda e: e.tensor_reduce(out=ss[:, 4:8], in_=ssq[:].rearrange("p (s q) -> p s q", q=4), axis=mybir.AxisListType.X, op=ALU.add),
                 reads=SSQ_ALL, writes=[("ss", 4 + s_) for s_ in range(NS)])
            S.op("scalar", lambda e: e.activation(out=rstd[:, 4:8], in_=ss[:, 4:8], func=AF.Sqrt, scale=1.0 / D, bias=EPS),
                 reads=[("ss", 4 + s_) for s_ in range(NS)], writes=[("rstd", 4 + s_) for s_ in range(NS)])
            S.op("vector", lambda e: e.reciprocal(out=rstd[:, 4:8], in_=rstd[:, 4:8]),
                 reads=[("rstd", 4 + s_) for s_ in range(NS)], writes=[("rstd", 4 + s_) for s_ in range(NS)])
            for s in range(NS):
                col = 4 + s
                S.op("vector", lambda e, s=s, col=col: e.scalar_tensor_tensor(
                    out=ot[s].v, in0=xt[s][:], scalar=rstd[:, col:col + 1], in1=grow_final.v, op0=ALU.mult, op1=ALU.mult),
                    reads=[("xt", s), ("rstd", col), grow_final.key()], writes=[ot[s].key()])
                r0 = (t - n_prefix) * T + s * 128
                S.op("sync", lambda e, s=s, r0=r0: e.dma_start(out=out[r0:r0 + 128, :], in_=ot[s].v),
                     reads=[ot[s].key()], writes=[("out", t, s)], dma_key=("st", s))
                load_x(t + 1, s)
        S.op("sync", lambda e: e.dma_start(out=ss[:, 0:1], in_=I["final_norm"][0:128].unsqueeze(1)), reads=[("out", t_, s_) for t_ in range(n_prefix, n_tiles) for s_ in range(NS)],
             writes=[("ss", 0)], dma_key="fin")
        S.emit()
    return nc


def kernel(**inputs):
    n_cores = 8
    nc = build_nc()
    shared = host_consts()
    shared.update(host_weight_layouts(inputs))
    for name, shape in INPUT_SPECS:
        if name == "x" or name.startswith("c_") or name.startswith("h_"):
            continue
        shared[name] = np.ascontiguousarray(np.asarray(inputs[name], dtype=np.float32))
    x = np.asarray(inputs["x"], dtype=np.float32)
    half = x.shape[1] // 2
    in_maps = []
    for c in range(n_cores):
        b, h = c // 2, c % 2
        m = dict(shared)
        xc = np.zeros((2 * half, x.shape[2]), np.float32)
        if h == 1:
            xc[:half] = x[b, :half]
        xc[half:] = x[b, h * half:(h + 1) * half]
        m["x"] = xc
        m["c_valid"] = np.full((128, 64), float(h), np.float32)
        in_maps.append(m)
    res = run_bass_kernel_spmd(nc, in_maps, core_ids=list(range(n_cores)))
    out = np.empty(x.shape, np.float32)
    for c in range(n_cores):
        b, h = c // 2, c % 2
        out[b, h * half:(h + 1) * half] = res.results[c]["out"]
    return out
```

```python
import contextlib
import math
import numpy as np
import concourse.bass as bass
import concourse.mybir as mybir
from concourse.bass_utils import run_bass_kernel_spmd

F32 = mybir.dt.float32
BF16 = mybir.dt.bfloat16
AF = mybir.ActivationFunctionType
ALU = mybir.AluOpType

D = 1024
DFF = 2816
NJ = DFF // 128
T = 512
NS = T // 128
EPS = 1e-6
Q = 8
NCH = T // Q
ENGS = ["sync", "gpsimd", "scalar", "vector", "tensor"]
DILS = (1, 4, 16)
MV = list(range(Q + 1)) + [Q * 2 ** i for i in range(7)] + [Q * (i + 1) for i in range(7)]


class Sched:
    def __init__(self, nc, stack):
        self.nc = nc
        self.stack = stack
        self.ops = {e: [] for e in ENGS}
        self.state = {}
        self.segs = []
        self.dma_count = {}
        self.final_keys = set()
        self.enabled = True

    def _ovl(self, k):
        if isinstance(k, tuple) and k and k[0] == "@":
            lo, hi = k[1], k[2]
            for b in (lo, hi):
                for (slo, shi) in list(self.segs):
                    if slo < b < shi:
                        st = self.state.pop(("@", slo, shi))
                        self.segs.remove((slo, shi))
                        for a_, b_ in ((slo, b), (b, shi)):
                            self.segs.append((a_, b_))
                            self.state[("@", a_, b_)] = [st[0], list(st[1])]
                        break
            covered = sorted((slo, shi) for (slo, shi) in self.segs if slo < hi and lo < shi)
            cur = lo
            for slo, shi in covered:
                if slo > cur:
                    self.segs.append((cur, slo))
                    self.state[("@", cur, slo)] = [None, []]
                cur = max(cur, shi)
            if cur < hi:
                self.segs.append((cur, hi))
                self.state[("@", cur, hi)] = [None, []]
            return [("@", slo, shi) for (slo, shi) in self.segs if lo <= slo and shi <= hi]
        if k not in self.state:
            self.state[k] = [None, []]
        return [k]

    def op(self, eng, fn, reads=(), writes=(), dma_key=None, final=False):
        if not self.enabled:
            return None
        rec = dict(eng=eng, fn=fn, deps=[], idx=len(self.ops[eng]), dma_key=dma_key, marked=False)
        seen = set()

        def add(d):
            if d is not None and id(d) not in seen:
                seen.add(id(d))
                rec["deps"].append(d)

        rk = [k2 for k in reads for k2 in self._ovl(k)]
        wk = [k2 for k in writes for k2 in self._ovl(k)]
        for k in rk:
            add(self.state[k][0])
        for k in wk:
            add(self.state[k][0])
            for r in self.state[k][1]:
                add(r)
        for k in wk:
            self.state[k][0] = rec
            self.state[k][1] = []
        wset = set(wk)
        for k in rk:
            if k not in wset:
                self.state[k][1].append(rec)
        best = {}
        for d_ in rec["deps"]:
            kk = ("dma", d_["dma_key"]) if d_["dma_key"] is not None else ("eng", d_["eng"])
            cur = best.get(kk)
            v = d_["dma_req"] if d_["dma_key"] is not None else d_["idx"]
            if cur is None or v > (cur["dma_req"] if cur["dma_key"] is not None else cur["idx"]):
                best[kk] = d_
        rec["deps"] = list(best.values())
        if dma_key is not None:
            c = self.dma_count.get(dma_key, 0) + 1
            self.dma_count[dma_key] = c
            rec["dma_req"] = 16 * c
            if final:
                self.final_keys.add(dma_key)
        self.ops[eng].append(rec)
        return rec

    @staticmethod
    def _needs_sync(d, rec):
        if d["dma_key"] is not None:
            return True
        if d["eng"] != rec["eng"]:
            return True
        if rec["eng"] == "tensor":
            return False
        return rec["idx"] - d["idx"] <= 3

    def emit(self):
        nc, stack = self.nc, self.stack
        ROT = 30000
        for e in ENGS:
            for rec in self.ops[e]:
                for d in rec["deps"]:
                    if d["dma_key"] is None and self._needs_sync(d, rec):
                        d["marked"] = True
        eng_sems = {}
        for e in ENGS:
            n = 0
            for rec in self.ops[e]:
                if rec["marked"]:
                    si, v = divmod(n, ROT)
                    key = (e, si)
                    if key not in eng_sems:
                        eng_sems[key] = stack.enter_context(nc.semaphore(f"s_{e}_{si}"))
                    rec["mark"] = (eng_sems[key], v + 1)
                    n += 1
        dma_sems = {}
        for i, k in enumerate(self.dma_count):
            dma_sems[k] = stack.enter_context(nc.semaphore(f"d_{i}"))

        def emit_engine(e, eng):
            waited = {}
            for rec in self.ops[e]:
                need = {}
                for d in rec["deps"]:
                    if not self._needs_sync(d, rec):
                        continue
                    if d["dma_key"] is not None:
                        k = d["dma_key"]
                        sem = dma_sems[k]
                        val = 16 * self.dma_count[k] if k in self.final_keys else d["dma_req"]
                    else:
                        sem, val = d["mark"]
                    if need.get(id(sem), (None, 0))[1] < val:
                        need[id(sem)] = (sem, val)
                for sid, (sem, val) in need.items():
                    if waited.get(sid, 0) < val:
                        eng.wait_ge(sem, val)
                        waited[sid] = val
                ins = rec["fn"](eng)
                if rec["dma_key"] is not None:
                    ins.then_inc(dma_sems[rec["dma_key"]], 16)
                elif rec["marked"]:
                    ins.then_inc(rec["mark"][0], 1)

        with nc.Block() as block:
            @block.sync
            def _(eng):
                emit_engine("sync", eng)

            @block.gpsimd
            def _(eng):
                emit_engine("gpsimd", eng)

            @block.scalar
            def _(eng):
                emit_engine("scalar", eng)

            @block.vector
            def _(eng):
                emit_engine("vector", eng)

            @block.tensor
            def _(eng):
                emit_engine("tensor", eng)


INPUT_SPECS = [
    ("x", [8192, 1024]),
    ("ffn1_norm", [1, 1024]), ("mix_norm", [1, 1024]), ("ffn2_norm", [1, 1024]), ("final_norm", [1024]),
    ("gate_bias", [1, 2048]), ("rel_bias_table", [32, 12]), ("ssm_a_re", [1, 32, 64]),
    ("ssm_a_im", [1, 32, 64]), ("ssm_log_dt", [1, 32]), ("ssm_b_re", [1, 32, 64, 16]),
    ("ssm_b_im", [1, 32, 64, 16]), ("ssm_c_re", [1, 32, 16, 64]), ("ssm_c_im", [1, 32, 16, 64]),
    ("ssm_d", [1, 512]),
    ("h_wgu_ffn1", [22, 128, 8, 256]), ("h_wd_ffn1", [2816, 1024]), ("h_wgu_ffn2", [22, 128, 8, 256]), ("h_wd_ffn2", [2816, 1024]),
    ("h_wqk", [12, 128, 8, 128]), ("h_wv", [3, 128, 8, 256]), ("h_wu", [4, 128, 8, 128]), ("h_wg", [16, 128, 8, 128]),
    ("h_wglu", [8, 128, 4, 128]), ("h_wab", [8, 128, 2, 128]), ("h_wsb", [8, 128, 4, 128]), ("h_wo", [1024, 1024]),
    ("c_ident", [128, 128]), ("c_flip", [128, 128]), ("c_bmask", [128, 128]), ("c_ohv", [3, 2, 33, 255]),
    ("c_valid", [128, 64]), ("c_mvec", [128, len(MV), 16]),
]


def host_weight_layouts(inputs):
    f32 = lambda a: np.asarray(a, dtype=np.float32)

    def tiles(w, kt, ncol, width):
        return np.ascontiguousarray(w.reshape(kt, 128, ncol, width).transpose(2, 1, 0, 3))

    out = {}
    for f in ("ffn1", "ffn2"):
        g = tiles(f32(inputs[f + "_w_gate"])[0], 8, 22, 128)
        u = tiles(f32(inputs[f + "_w_up"])[0], 8, 22, 128)
        out["h_wgu_" + f] = np.ascontiguousarray(np.concatenate([g, u], axis=3))
        out["h_wd_" + f] = np.ascontiguousarray(f32(inputs[f + "_w_down"])[0])
    win = f32(inputs["w_in"])[0]
    out["h_wqk"] = tiles(win[:, 0:1536], 8, 12, 128)
    out["h_wv"] = tiles(win[:, 1536:2304], 8, 3, 256)
    out["h_wu"] = tiles(win[:, 2304:2816], 8, 4, 128)
    out["h_wg"] = tiles(win[:, 2816:4864], 8, 16, 128)
    out["h_wglu"] = tiles(f32(inputs["ssm_w_glu"])[0], 4, 8, 128)
    out["h_wab"] = tiles(f32(inputs["w_attn_branch"])[0], 2, 8, 128)
    out["h_wsb"] = tiles(f32(inputs["w_ssm_branch"])[0], 4, 8, 128)
    out["h_wo"] = np.ascontiguousarray(f32(inputs["w_out"])[0])
    return out


def _t5_bucket(dist):
    if dist < 16:
        return int(dist)
    d = np.float32(max(dist, 1))
    v = np.log(d / np.float32(16)) / np.float32(math.log(2048 / 16)) * np.float32(16)
    return int(min(16 + int(np.float32(v)), 31))


def host_consts():
    ident = np.eye(128, dtype=np.float32)
    flip = np.ascontiguousarray(ident[::-1])
    r = np.arange(128)
    bmask = (r[:, None] // 16 == r[None, :] // 16).astype(np.float32)
    ohv = np.zeros((3, 2, 33, 255), np.float32)
    for g, d in enumerate(DILS):
        for pc in range(2):
            for n in range(255):
                if pc == 1:
                    valid, steps = n >= 127, n - 127
                else:
                    valid, steps = n <= 127, n + 1
                b = _t5_bucket(steps * d) if valid else 32
                ohv[g, pc, b, n] = 1.0
    mvec = np.ascontiguousarray(np.broadcast_to(np.asarray(MV, np.float32)[None, :, None], (128, len(MV), 16)))
    return {"c_ident": ident, "c_flip": flip, "c_bmask": bmask, "c_ohv": ohv, "c_mvec": mvec}


def build_nc(n_tiles=16, stages=("ffn1", "mix", "ffn2"), stop_after=None, n_prefix=8):
    nc = bass.Bass("TRN2", target_bir_lowering=False)
    I = {}
    for name, shape in INPUT_SPECS:
        shp = list(shape)
        if name == "x":
            shp = [n_tiles * T, 1024]
        I[name] = nc.dram_tensor(name, shp, F32, kind="ExternalInput").ap()
    out = nc.dram_tensor("out", [(n_tiles - n_prefix) * T, 1024], F32, kind="ExternalOutput").ap()

    with contextlib.ExitStack() as stack:
        def sb(name, shape, dt):
            return stack.enter_context(nc.sbuf_tensor(name, shape, dt))

        def ps(name, shape, dt):
            return stack.enter_context(nc.psum_tensor(name, shape, dt))

        def dram(name, shape, dt):
            t = nc.dram_tensor(name, shape, dt)
            return t.ap() if hasattr(t, "ap") else t

        S = Sched(nc, stack)
        PI = math.pi

        def CP(name):
            if name == stop_after:
                S.enabled = False

        Wgu = {f: dram(f"wgu_{f}", [NJ, 128, 8, 256], BF16) for f in ("ffn1", "ffn2")}
        Wd = {f: dram(f"wd_{f}", [DFF, D], BF16) for f in ("ffn1", "ffn2")}
        Wqk = dram("wqk", [12, 128, 8, 128], BF16)
        Wv = dram("wv", [3, 128, 8, 256], BF16)
        Wu = dram("wu", [4, 128, 8, 128], BF16)
        Wg = dram("wg", [16, 128, 8, 128], BF16)
        Wglu = dram("wglu", [8, 128, 4, 128], BF16)
        Wab = dram("wab", [8, 128, 2, 128], BF16)
        Wsb = dram("wsb", [8, 128, 4, 128], BF16)
        Wo = dram("wo", [D, D], BF16)
        evecD = dram("evecd", [4, 6, 256], F32)

        ARENA = 58 * 1024
        arena = sb("arena", [128, ARENA // 2], BF16)

        class AT:
            def __init__(self, off, shape, dt):
                self.off, self.shape, self.dt = off, shape, dt
                self.esz = 4 if dt == F32 else 2
                n = int(np.prod(shape[1:]))
                self.nbytes = n * self.esz
                v = arena[0:shape[0], off // 2: off // 2 + self.nbytes // 2]
                if dt == F32:
                    v = v.bitcast(F32)
                if len(shape) == 3:
                    v = v.rearrange("p (a b) -> p a b", a=shape[1])
                elif len(shape) == 4:
                    v = v.rearrange("p (a b c) -> p a b c", a=shape[1], b=shape[2])
                self.v = v

            def key(self):
                return ("@", self.off, self.off + self.nbytes)

            def sub(self, i, n):
                w = self.nbytes // n
                return ("@", self.off + i * w, self.off + (i + 1) * w)

        o = 0
        hT = AT(o, [128, NJ, T], BF16); o += hT.nbytes
        wd_sb = [AT(o + i * 11264, [128, NJ, 256], BF16) for i in range(2)]; o += 2 * 11264
        wgu_sb = [AT(o + i * 4096, [128, 8, 256], BF16) for i in range(3)]; o += 3 * 4096
        assert o <= ARENA, o
        ot = [AT(s * 4096, [128, D], F32) for s in range(NS)]
        grow_final = AT(22528, [128, D], F32)
        o = 0
        Hs = AT(o, [128, NCH + 1, 48], F32); o += 12544
        wv_sb = [AT(o + i * 4096, [128, 8, 256], BF16) for i in range(2)]
        mT = AT(o, [128, 8, T], BF16); o += 8192
        QT = AT(o, [128, 6, T], BF16)
        wo_sb = [AT(o + i * 4096, [128, 8, 256], BF16) for i in range(2)]
        o += QT.nbytes
        Sb = [AT(o + i * 2048, [128, 16, NCH], BF16) for i in range(2)]; o += 4096
        uT = AT(o, [128, 4, T], BF16); o += uT.nbytes
        geT = AT(o, [128, 4, T], BF16); o += geT.nbytes
        ysb = [AT(o + i * 2048, [128, T], F32) for i in range(2)]; o += 4096
        ys2T = AT(o, [128, 4, T], BF16); o += 4096
        oT = AT(o, [128, 2, T], BF16); o += 2048
        PT = [AT(o + i * 1024, [128, T], BF16) for i in range(2)]; o += 2048
        wqk_sb = [AT(o + i * 2048, [128, 8, 128], BF16) for i in range(2)]; o += 4096
        bs1 = AT(o, [128, 8, 32], F32); o += 1024
        bs2 = AT(o, [128, 8, 32], F32); o += 1024
        assert o <= ARENA, o
        wg_sb = [AT(wqk_sb[0].off, [128, 8, 128], BF16), AT(wqk_sb[1].off, [128, 8, 128], BF16),
                 AT(uT.off, [128, 8, 128], BF16), AT(uT.off + 2048, [128, 8, 128], BF16)]
        wglu_sb = [AT(QT.off + i * 1024, [128, 4, 128], BF16) for i in range(4)]
        o = 0
        zt = {}
        for nm in ("bbz_re", "bbz_im", "ctz_re", "ctz_im", "tmpz", "tmpz2"):
            zt[nm] = AT(o, [128, 16, 2, 16], F32); o += 2048
        for nm in ("abz_re0", "abz_im0", "caz_re0", "caz_im0", "abz_re1", "abz_im1", "caz_re1", "caz_im1", "ctzb_re", "nctzb_im"):
            zt[nm] = AT(o, [128, 16, 2, 16], BF16); o += 1024
        for nm in ("b_re", "b_im", "ct_re", "ct_im", "tA", "tB"):
            zt[nm] = AT(o, [128, 16, 16], F32); o += 1024
        cnat = [AT(o + i * 512, [128, 128], F32) for i in range(4)]; o += 2048
        cnatb = [AT(o + i * 256, [128, 128], BF16) for i in range(4)]; o += 1024
        Rh = [AT(o + i * 512, [128, 128], F32) for i in range(4)]; o += 4 * 512
        Rhb = [AT(o + i * 256, [128, 128], BF16) for i in range(4)]; o += 4 * 256
        ohv_sb = AT(o, [64, 6, 255], F32); o += 6 * 255 * 4
        ohvb = AT(o, [64, 6, 256], BF16); o += 6 * 256 * 2
        evec_sb = AT(o, [4, 6, 256], F32); o += 6 * 256 * 4
        pw = {}
        for nm in ("mv", "mag", "a1", "a2", "k"):
            pw[nm] = AT(o, [128, len(MV), 16], F32); o += len(MV) * 64
        assert o <= ARENA, o

        xt = [sb(f"xt{s}", [128, D], F32) for s in range(NS)]
        xn = [sb(f"xn{i}", [128, D], BF16) for i in range(1)]
        junk = sb("junk", [128, D], BF16)
        xT = sb("xT", [128, 8, T], BF16)
        sg = [sb(f"sg{i}", [128, T], F32) for i in range(2)]
        gT = {n: sb(f"gT_{n}", [128, 8], F32) for n in ("ffn1", "mix", "ffn2")}
        ss = sb("ss", [128, 8], F32)
        ssq = sb("ssq", [128, 16], F32)
        junkq = sb("junkq", [128, 256], BF16)
        rstd = sb("rstd", [128, 8], F32)
        idf = sb("idf", [128, 128], F32)
        idb = sb("idb", [128, 128], BF16)
        flipf = sb("flipf", [128, 128], F32)
        flipb = sb("flipb", [128, 128], BF16)
        tabb = sb("tabb", [64, 12], BF16)
        bmask = sb("bmask", [128, 128], F32)
        ones_bf = sb("ones_bf", [128, 64], BF16)
        valid_f = sb("valid_f", [128, 64], F32)
        valid_bf = sb("valid_bf", [128, 64], BF16)
        KT = {0: [sb(f"kt0_{i}", [128, 2 * T], BF16) for i in range(2)],
              1: [sb(f"kt1_{i}", [128, 2 * T], BF16) for i in range(2)],
              2: [sb(f"kt2_{i}", [128, 4096], BF16) for i in range(2)]}
        VR = {0: sb("v0", [128, 8, 256], BF16), 1: sb("v1", [128, 8, 256], BF16), 2: sb("v2", [128, 32, 256], BF16)}
        Ecat = [[sb(f"ecat{g}_{h}", [128, 256], BF16) for h in range(4)] for g in range(3)]
        Gt = sb("Gt", [128, 4, Q, 128], BF16)
        Bt = sb("Bt", [128, 4, Q, 2, 128], BF16)
        Ct = sb("Ct", [128, 16, Q, 2, 32], BF16)
        AR2 = sb("AR2", [128, 32], F32)
        AI2 = sb("AI2", [128, 32], F32)
        APR = sb("APR", [128, 7, 32], F32)
        API = sb("API", [128, 7, 32], F32)
        APR3 = sb("APR3", [128, 7, 32], F32)
        API3 = sb("API3", [128, 7, 32], F32)
        Pm = sb("Pm", [128, Q + 1, 2, 16], F32)
        sc = {nm: sb("sc_" + nm, [128, 16], F32) for nm in ("ar", "ai", "ldt", "dt", "lr", "li", "mag", "a1", "a2", "s", "c", "den", "xr", "cre", "cim", "t1", "t2")}
        dsk = sb("dsk", [128, 4], F32)
        gbT = sb("gbT", [128, 16], F32)
        tabaug = sb("tabaug", [64, 12], F32)
        Scarry = sb("Scarry", [128, 48], F32)
        rt1 = [sb(f"rt1_{i}", [128, 32], F32) for i in range(2)]
        rt2 = [sb(f"rt2_{i}", [128, 32], F32) for i in range(2)]
        rden = [sb(f"rden{i}", [128, T], F32) for i in range(2)]
        wu_sb = [sb(f"wu_sb{i}", [128, 8, 128], BF16) for i in range(2)]
        wab_sb = [sb(f"wab_sb{i}", [128, 2, 128], BF16) for i in range(2)]
        wsb_sb = [sb(f"wsb_sb{i}", [128, 4, 128], BF16) for i in range(2)]

        pall = ps("pall", [128, 4096], F32)
        pb = [pall[:, i * 512:(i + 1) * 512] for i in range(8)]

        def PB(i):
            return ("pb", i)

        def cast(out_ap, in_ap, wkey, grp):
            S.op("gpsimd", lambda e: e.dma_start(out=out_ap, in_=in_ap), writes=[wkey], dma_key=grp, final=True)

        def cast_ffn(f):
            for j in range(NJ):
                cast(Wgu[f][j], I["h_wgu_" + f][j], ("Wgu", f, j), ("su", f, j // 2))
            for j in range(NJ):
                cast(Wd[f][j * 128:(j + 1) * 128, :], I["h_wd_" + f][j * 128:(j + 1) * 128, :], ("Wd", f, j), ("suwd", f))

        if "ffn1" in stages:
            cast_ffn("ffn1")
        if "mix" in stages:
            for c in range(4):
                cast(Wu[c], I["h_wu"][c], ("Wu", c), "su_u")
            for c in range(6, 12):
                cast(Wqk[c], I["h_wqk"][c], ("Wqk", c), "su_kv")
            for g in range(3):
                cast(Wv[g], I["h_wv"][g], ("Wv", g), "su_kv")
            for c in range(6):
                cast(Wqk[c], I["h_wqk"][c], ("Wqk", c), "su_mix")
            for c in range(16):
                cast(Wg[c], I["h_wg"][c], ("Wg", c), "su_mix")
            for c in range(8):
                cast(Wglu[c], I["h_wglu"][c], ("Wglu", c), "su_mix")
                cast(Wab[c], I["h_wab"][c], ("Wab", c), "su_mix")
                cast(Wsb[c], I["h_wsb"][c], ("Wsb", c), "su_mix")
            for k in range(8):
                cast(Wo[k * 128:(k + 1) * 128, :], I["h_wo"][k * 128:(k + 1) * 128, :], ("Wo", k), "su_mix")
        if "ffn2" in stages:
            cast_ffn("ffn2")

        def ld(eng, out_ap, in_ap, wkey, dkey):
            S.op(eng, lambda e: e.dma_start(out=out_ap, in_=in_ap), writes=[wkey], dma_key=dkey)

        for n, src in (("ffn1", I["ffn1_norm"][0]), ("mix", I["mix_norm"][0]), ("ffn2", I["ffn2_norm"][0])):
            S.op("sync", lambda e, n=n, src=src: e.dma_start(out=gT[n][:], in_=src.rearrange("(k p) -> p k", p=128), allow_slow_non_contiguous=True),
                 writes=[("gT", n)], dma_key=("gT", n))
        ld("sync", idf[:], I["c_ident"], "idf", "idf")
        ld("sync", flipf[:], I["c_flip"], "flipf", "flipf")
        ld("sync", bmask[:], I["c_bmask"], "bmask", "bmask")
        S.op("sync", lambda e: e.dma_start(out=dsk[:], in_=I["ssm_d"][0].rearrange("(c p) -> p c", p=128), allow_slow_non_contiguous=True),
             writes=["dsk"], dma_key="dsk")
        S.op("sync", lambda e: e.dma_start(out=gbT[:], in_=I["gate_bias"][0].rearrange("(c p) -> p c", p=128), allow_slow_non_contiguous=True),
             writes=["gbT"], dma_key="gbT")
        S.op("vector", lambda e: e.tensor_copy(out=idb[:], in_=idf[:]), reads=["idf"], writes=["idb"])
        S.op("vector", lambda e: e.memset(ones_bf[:], 1.0), writes=["ones"])
        ld("sync", valid_f[:], I["c_valid"], "valid_f", "valid_f")
        S.op("vector", lambda e: e.tensor_copy(out=valid_bf[:], in_=valid_f[:]), reads=["valid_f"], writes=["valid"])
        for g in range(3):
            for i in range(2):
                S.op("gpsimd", lambda e, g=g, i=i: e.memset(KT[g][i][:], 0.0), writes=[("KT", g, i, s_) for s_ in range(8)])
            S.op("gpsimd", lambda e, g=g: e.memset(VR[g][:], 0.0), writes=[("V", g, i_) for i_ in range(32)])

        cnt = {}
        prefetch_x = [None]
        cur_tile = [0]

        def rot(name, n):
            v = cnt.get(name, 0)
            cnt[name] = v + 1
            return v % n

        def V(fn, reads, writes):
            S.op("vector", fn, reads=reads, writes=writes)

        def ACT(fn, reads, writes):
            S.op("scalar", fn, reads=reads, writes=writes)

        if "mix" in stages:
            S.op("sync", lambda e: e.dma_start(out=sc["ar"][:], in_=I["ssm_a_re"][0].rearrange("(p e) n -> (e n) p", e=2), allow_slow_non_contiguous=True),
                 writes=["sc_ar"], dma_key="sc_ar")
            S.op("sync", lambda e: e.dma_start(out=sc["ai"][:], in_=I["ssm_a_im"][0].rearrange("(p e) n -> (e n) p", e=2), allow_slow_non_contiguous=True),
                 writes=["sc_ai"], dma_key="sc_ai")
            for e_ in range(2):
                S.op("sync", lambda e, e_=e_: e.dma_start(out=sc["ldt"][e_ * 64:(e_ + 1) * 64, :], in_=I["ssm_log_dt"][0:1, e_:32:2].to_broadcast([64, 16]),
                                                         allow_slow_non_contiguous=True),
                     writes=[("sc_ldt", e_)], dma_key=("sc_ldt", e_))
            for nm, src in (("b_re", "ssm_b_re"), ("b_im", "ssm_b_im")):
                S.op("sync", lambda e, nm=nm, src=src: e.dma_start(out=zt[nm].v, in_=I[src][0].rearrange("(p e) n c -> (e n) p c", e=2)),
                     writes=[zt[nm].key()], dma_key=("ld", nm))
            for ri, src in enumerate(("ssm_c_re", "ssm_c_im")):
                for hs in range(2):
                    ci = ri * 2 + hs
                    for p8 in range(8):
                        for e_ in range(2):
                            gidx = 2 * (hs * 8 + p8) + e_
                            S.op("sync", lambda e, ci=ci, p8=p8, e_=e_, gidx=gidx, src=src: e.dma_start(
                                out=cnat[ci].v[p8 * 16:(p8 + 1) * 16, e_ * 64:(e_ + 1) * 64], in_=I[src][0, gidx]),
                                writes=[("cnat", ci, p8, e_)], dma_key=("cnat", ci), final=True)
            for ri, nm in enumerate(("ct_re", "ct_im")):
                for hs in range(2):
                    ci = ri * 2 + hs
                    bi = rot("pbs", 2)
                    V(lambda e, ci=ci: e.tensor_copy(out=cnatb[ci].v, in_=cnat[ci].v),
                      [cnat[ci].key()] + [("cnat", ci, a_, b_) for a_ in range(8) for b_ in range(2)], [cnatb[ci].key()])
                    S.op("tensor", lambda e, ci=ci, bi=bi: e.transpose(pb[bi].bitcast(BF16)[:, 0:128], cnatb[ci].v, idb[:]),
                         reads=[cnatb[ci].key(), "idb"], writes=[PB(bi)])
                    ACT(lambda e, nm=nm, hs=hs, bi=bi: e.copy(
                        out=zt[nm].v[:, hs * 8:(hs + 1) * 8, :], in_=pb[bi].bitcast(BF16)[:, 0:128].rearrange("q (p c) -> q p c", p=8)),
                        [PB(bi)], [zt[nm].key()])

            ACT(lambda e: e.activation(out=sc["dt"][:], in_=sc["ldt"][:], func=AF.Exp), [("sc_ldt", 0), ("sc_ldt", 1)], ["sc_dt"])
            V(lambda e: e.tensor_tensor(out=sc["lr"][:], in0=sc["ar"][:], in1=sc["dt"][:], op=ALU.mult), ["sc_ar", "sc_dt"], ["sc_lr"])
            V(lambda e: e.tensor_tensor(out=sc["li"][:], in0=sc["ai"][:], in1=sc["dt"][:], op=ALU.mult), ["sc_ai", "sc_dt"], ["sc_li"])
            MAGIC = 12582912.0
            NM = len(MV)
            ld("sync", pw["mv"].v, I["c_mvec"], pw["mv"].key(), "c_mvec")
            lrb = sc["lr"][:].unsqueeze(1).to_broadcast([128, NM, 16])
            lib = sc["li"][:].unsqueeze(1).to_broadcast([128, NM, 16])
            V(lambda e: e.tensor_tensor(out=pw["mag"].v, in0=pw["mv"].v, in1=lrb, op=ALU.mult), [pw["mv"].key(), "sc_lr"], [pw["mag"].key()])
            ACT(lambda e: e.activation(out=pw["mag"].v, in_=pw["mag"].v, func=AF.Exp), [pw["mag"].key()], [pw["mag"].key()])
            V(lambda e: e.tensor_tensor(out=pw["a1"].v, in0=pw["mv"].v, in1=lib, op=ALU.mult), [pw["mv"].key(), "sc_li"], [pw["a1"].key()])
            V(lambda e: e.tensor_scalar(out=pw["a2"].v, in0=pw["a1"].v, scalar1=0.5 * PI, scalar2=None, op0=ALU.add), [pw["a1"].key()], [pw["a2"].key()])
            for an in ("a1", "a2"):
                V(lambda e, an=an: e.tensor_scalar(out=pw["k"].v, in0=pw[an].v, scalar1=1.0 / (2 * PI), scalar2=MAGIC, op0=ALU.mult, op1=ALU.add), [pw[an].key()], [pw["k"].key()])
                V(lambda e: e.tensor_scalar(out=pw["k"].v, in0=pw["k"].v, scalar1=-MAGIC, scalar2=None, op0=ALU.add), [pw["k"].key()], [pw["k"].key()])
                V(lambda e, an=an: e.scalar_tensor_tensor(out=pw[an].v, in0=pw["k"].v, scalar=-2 * PI, in1=pw[an].v, op0=ALU.mult, op1=ALU.add), [pw["k"].key(), pw[an].key()], [pw[an].key()])
                V(lambda e, an=an: e.tensor_scalar(out=pw[an].v, in0=pw[an].v, scalar1=-PI, scalar2=PI, op0=ALU.max, op1=ALU.min), [pw[an].key()], [pw[an].key()])
                ACT(lambda e, an=an: e.activation(out=pw[an].v, in_=pw[an].v, func=AF.Sin), [pw[an].key()], [pw[an].key()])
            V(lambda e: e.tensor_tensor(out=pw["a2"].v, in0=pw["a2"].v, in1=pw["mag"].v, op=ALU.mult), [pw["a2"].key(), pw["mag"].key()], [pw["a2"].key()])
            V(lambda e: e.tensor_tensor(out=pw["a1"].v, in0=pw["a1"].v, in1=pw["mag"].v, op=ALU.mult), [pw["a1"].key(), pw["mag"].key()], [pw["a1"].key()])
            PRE, PIM = pw["a2"], pw["a1"]
            PMK = [("Pm", m_) for m_ in range(Q + 1)]
            V(lambda e: e.tensor_copy(out=Pm[:, :, 0, :], in_=PRE.v[:, 0:Q + 1, :]), [PRE.key()], PMK)
            V(lambda e: e.tensor_copy(out=Pm[:, :, 1, :], in_=PIM.v[:, 0:Q + 1, :]), [PIM.key()], PMK)
            for (TR, TI, o0, kn) in ((APR, API, Q + 1, "APW"), (APR3, API3, Q + 8, "APW3")):
                KK = [(kn, i_) for i_ in range(7)]
                V(lambda e, TR=TR, o0=o0: e.tensor_copy(out=TR[:, :, 0:16], in_=PRE.v[:, o0:o0 + 7, :]), [PRE.key()], KK)
                V(lambda e, TR=TR, o0=o0: e.tensor_copy(out=TR[:, :, 16:32], in_=PRE.v[:, o0:o0 + 7, :]), [PRE.key()], KK)
                V(lambda e, TI=TI, o0=o0: e.tensor_copy(out=TI[:, :, 16:32], in_=PIM.v[:, o0:o0 + 7, :]), [PIM.key()], KK)
                V(lambda e, TI=TI, o0=o0: e.tensor_scalar(out=TI[:, :, 0:16], in0=PIM.v[:, o0:o0 + 7, :], scalar1=-1.0, scalar2=None, op0=ALU.mult), [PIM.key()], KK)
            V(lambda e: e.tensor_tensor(out=sc["den"][:], in0=sc["ar"][:], in1=sc["ar"][:], op=ALU.mult), ["sc_ar"], ["sc_den"])
            V(lambda e: e.tensor_tensor(out=sc["t1"][:], in0=sc["ai"][:], in1=sc["ai"][:], op=ALU.mult), ["sc_ai"], ["sc_t1"])
            V(lambda e: e.tensor_tensor(out=sc["den"][:], in0=sc["den"][:], in1=sc["t1"][:], op=ALU.add), ["sc_den", "sc_t1"], ["sc_den"])
            V(lambda e: e.reciprocal(out=sc["den"][:], in_=sc["den"][:]), ["sc_den"], ["sc_den"])
            V(lambda e: e.tensor_scalar(out=sc["xr"][:], in0=Pm[:, 1, 0, :], scalar1=-1.0, scalar2=None, op0=ALU.add), [("Pm", 1)], ["sc_xr"])
            V(lambda e: e.tensor_tensor(out=sc["t1"][:], in0=sc["xr"][:], in1=sc["ar"][:], op=ALU.mult), ["sc_xr", "sc_ar"], ["sc_t1"])
            V(lambda e: e.tensor_tensor(out=sc["t2"][:], in0=Pm[:, 1, 1, :], in1=sc["ai"][:], op=ALU.mult), [("Pm", 1), "sc_ai"], ["sc_t2"])
            V(lambda e: e.tensor_tensor(out=sc["t1"][:], in0=sc["t1"][:], in1=sc["t2"][:], op=ALU.add), ["sc_t1", "sc_t2"], ["sc_t1"])
            V(lambda e: e.tensor_tensor(out=sc["cre"][:], in0=sc["t1"][:], in1=sc["den"][:], op=ALU.mult), ["sc_t1", "sc_den"], ["sc_cre"])
            V(lambda e: e.tensor_tensor(out=sc["t1"][:], in0=Pm[:, 1, 1, :], in1=sc["ar"][:], op=ALU.mult), [("Pm", 1), "sc_ar"], ["sc_t1"])
            V(lambda e: e.tensor_tensor(out=sc["t2"][:], in0=sc["xr"][:], in1=sc["ai"][:], op=ALU.mult), ["sc_xr", "sc_ai"], ["sc_t2"])
            V(lambda e: e.tensor_tensor(out=sc["t1"][:], in0=sc["t1"][:], in1=sc["t2"][:], op=ALU.subtract), ["sc_t1", "sc_t2"], ["sc_t1"])
            V(lambda e: e.tensor_tensor(out=sc["cim"][:], in0=sc["t1"][:], in1=sc["den"][:], op=ALU.mult), ["sc_t1", "sc_den"], ["sc_cim"])
            V(lambda e: e.tensor_copy(out=AR2[:, 0:16], in_=Pm[:, Q, 0, :]), [("Pm", Q)], ["AR2"])
            V(lambda e: e.tensor_copy(out=AR2[:, 16:32], in_=Pm[:, Q, 0, :]), [("Pm", Q)], ["AR2"])
            V(lambda e: e.tensor_scalar(out=AI2[:, 0:16], in0=Pm[:, Q, 1, :], scalar1=-1.0, scalar2=None, op0=ALU.mult), [("Pm", Q)], ["AI2"])
            V(lambda e: e.tensor_copy(out=AI2[:, 16:32], in_=Pm[:, Q, 1, :]), [("Pm", Q)], ["AI2"])

            def bc16(ap2):
                return ap2.unsqueeze(2).to_broadcast([128, 16, 16])

            def bc32(ap2):
                return ap2.unsqueeze(2).to_broadcast([128, 16, 32])

            for nm in ("bbz_re", "bbz_im", "ctz_re", "ctz_im", "ctzb_re", "nctzb_im"):
                V(lambda e, nm=nm: e.memset(zt[nm].v, 0.0), [], [zt[nm].key()])
            tA, tB = zt["tA"], zt["tB"]

            def interleave(dst, src_at, scale=None):
                for e_ in range(2):
                    if scale is None:
                        V(lambda e, e_=e_: e.tensor_copy(out=zt[dst].v[e_ * 64:(e_ + 1) * 64, :, e_, :], in_=src_at.v[e_ * 64:(e_ + 1) * 64, :, :]),
                          [src_at.key()], [zt[dst].key()])
                    else:
                        V(lambda e, e_=e_: e.tensor_scalar(out=zt[dst].v[e_ * 64:(e_ + 1) * 64, :, e_, :], in0=src_at.v[e_ * 64:(e_ + 1) * 64, :, :],
                                                          scalar1=scale, scalar2=None, op0=ALU.mult),
                          [src_at.key()], [zt[dst].key()])

            V(lambda e: e.tensor_tensor(out=tA.v, in0=zt["b_re"].v, in1=bc16(sc["cre"][:]), op=ALU.mult), [zt["b_re"].key(), "sc_cre"], [tA.key()])
            V(lambda e: e.tensor_tensor(out=tB.v, in0=zt["b_im"].v, in1=bc16(sc["cim"][:]), op=ALU.mult), [zt["b_im"].key(), "sc_cim"], [tB.key()])
            V(lambda e: e.tensor_tensor(out=tA.v, in0=tA.v, in1=tB.v, op=ALU.subtract), [tA.key(), tB.key()], [tA.key()])
            interleave("bbz_re", tA)
            V(lambda e: e.tensor_tensor(out=tA.v, in0=zt["b_im"].v, in1=bc16(sc["cre"][:]), op=ALU.mult), [zt["b_im"].key(), "sc_cre"], [tA.key()])
            V(lambda e: e.tensor_tensor(out=tB.v, in0=zt["b_re"].v, in1=bc16(sc["cim"][:]), op=ALU.mult), [zt["b_re"].key(), "sc_cim"], [tB.key()])
            V(lambda e: e.tensor_tensor(out=tA.v, in0=tA.v, in1=tB.v, op=ALU.add), [tA.key(), tB.key()], [tA.key()])
            interleave("bbz_im", tA)
            interleave("ctz_re", zt["ct_re"])
            interleave("ctz_im", zt["ct_im"])
            interleave("ctzb_re", zt["ct_re"])
            interleave("nctzb_im", zt["ct_im"], scale=-1.0)

            def flat(at):
                return at.v.rearrange("q p e c -> q (p e c)")

            def v3(at):
                return at.v.rearrange("q p e c -> q p (e c)")

            tmpz, tmpz2 = zt["tmpz"], zt["tmpz2"]

            def cmul(m, dre, dim, sre, sim):
                pre = Pm[:, m, 0, :]
                pim = Pm[:, m, 1, :]
                V(lambda e: e.tensor_tensor(out=v3(tmpz), in0=v3(sre), in1=bc32(pre), op=ALU.mult), [sre.key(), ("Pm", m)], [tmpz.key()])
                V(lambda e: e.tensor_tensor(out=v3(tmpz2), in0=v3(sim), in1=bc32(pim), op=ALU.mult), [sim.key(), ("Pm", m)], [tmpz2.key()])
                V(lambda e: e.tensor_tensor(out=v3(dre), in0=v3(tmpz), in1=v3(tmpz2), op=ALU.subtract), [tmpz.key(), tmpz2.key()], [dre.key()])
                V(lambda e: e.tensor_tensor(out=v3(tmpz), in0=v3(sim), in1=bc32(pre), op=ALU.mult), [sim.key(), ("Pm", m)], [tmpz.key()])
                V(lambda e: e.tensor_tensor(out=v3(tmpz2), in0=v3(sre), in1=bc32(pim), op=ALU.mult), [sre.key(), ("Pm", m)], [tmpz2.key()])
                V(lambda e: e.tensor_tensor(out=v3(dim), in0=v3(tmpz), in1=v3(tmpz2), op=ALU.add), [tmpz.key(), tmpz2.key()], [dim.key()])

            for m in range(Q + 1):
                par = m % 2
                abr, abi, car, cai = (zt[f"abz_re{par}"], zt[f"abz_im{par}"], zt[f"caz_re{par}"], zt[f"caz_im{par}"])
                if m <= Q - 1:
                    cmul(m, abr, abi, zt["bbz_re"], zt["bbz_im"])
                cmul(m, car, cai, zt["ctz_re"], zt["ctz_im"])
                if m <= Q - 1:
                    for ct in range(4):
                        bi = rot("pbs", 2)
                        S.op("tensor", lambda e, ct=ct, bi=bi, abr=abr: e.matmul(pb[bi][:, 0:128], lhsT=flat(abr)[:, ct * 128:(ct + 1) * 128],
                                                                                 rhs=flat(zt["ctzb_re"])[:, ct * 128:(ct + 1) * 128], start=True, stop=False),
                             reads=[abr.key(), zt["ctzb_re"].key()], writes=[PB(bi)])
                        S.op("tensor", lambda e, ct=ct, bi=bi, abi=abi: e.matmul(pb[bi][:, 0:128], lhsT=flat(abi)[:, ct * 128:(ct + 1) * 128],
                                                                                 rhs=flat(zt["nctzb_im"])[:, ct * 128:(ct + 1) * 128], start=False, stop=True),
                             reads=[abi.key(), zt["nctzb_im"].key()], writes=[PB(bi)])
                        if m == 0:
                            V(lambda e, bi=bi: e.tensor_tensor(out=sg[0][:, 0:128], in0=pb[bi][:, 0:128], in1=bmask[:], op=ALU.mult), [PB(bi), "bmask"], [("sg", 0)])
                            V(lambda e, ct=ct: e.scalar_tensor_tensor(out=Gt[:, ct, 0, :], in0=idf[:], scalar=dsk[:, ct:ct + 1], in1=sg[0][:, 0:128],
                                                                      op0=ALU.mult, op1=ALU.add), [("sg", 0), "idf", "dsk"], [("Gt", ct, 0)])
                        else:
                            V(lambda e, bi=bi, ct=ct, m=m: e.tensor_tensor(out=Gt[:, ct, m, :], in0=pb[bi][:, 0:128], in1=bmask[:], op=ALU.mult),
                              [PB(bi), "bmask"], [("Gt", ct, m)])
                        for ri, ab in enumerate((abr, abi)):
                            bi2 = rot("pbs", 2)
                            S.op("tensor", lambda e, ct=ct, bi2=bi2, ab=ab: e.transpose(pb[bi2].bitcast(BF16)[:, 0:128], flat(ab)[:, ct * 128:(ct + 1) * 128], idb[:]),
                                 reads=[ab.key(), "idb"], writes=[PB(bi2)])
                            ACT(lambda e, ct=ct, bi2=bi2, ri=ri, m=m: e.copy(out=Bt[:, ct, Q - 1 - m, ri, :], in_=pb[bi2].bitcast(BF16)[:, 0:128]), [PB(bi2)], [("Bt", ct)])
                if m >= 1:
                    ACT(lambda e, m=m, car=car: e.copy(out=Ct[:, :, m - 1, 0, :], in_=v3(car)), [car.key()], ["Ct"])
                    ACT(lambda e, m=m, cai=cai: e.mul(out=Ct[:, :, m - 1, 1, :], in_=v3(cai), mul=-1.0), [cai.key()], ["Ct"])

            CP("setup_ssm")
            V(lambda e: e.memset(tabaug[:], 0.0), [], ["tabaug"])
            ld("sync", tabaug[0:32, :], I["rel_bias_table"], "tabaug", "tabaug")
            V(lambda e: e.memset(tabaug[32:33, :], -30000.0), ["tabaug"], ["tabaug"])
            V(lambda e: e.memset(ohv_sb.v, 0.0), [], [ohv_sb.key()])
            S.op("sync", lambda e: e.dma_start(out=ohv_sb.v[0:33], in_=I["c_ohv"].rearrange("g c b n -> b (g c) n")), writes=[ohv_sb.key()], dma_key="ohv")
            V(lambda e: e.memset(evec_sb.v, 0.0), [], [evec_sb.key()])
            V(lambda e: e.tensor_copy(out=tabb[:], in_=tabaug[:]), ["tabaug"], ["tabb"])
            V(lambda e: e.tensor_copy(out=ohvb.v[:, :, 0:255], in_=ohv_sb.v), [ohv_sb.key()], [ohvb.key()])
            V(lambda e: e.tensor_copy(out=flipb[:], in_=flipf[:]), ["flipf"], ["flipb"])
            for g in range(3):
                bi = 2 + g
                for pc in range(2):
                    S.op("tensor", lambda e, g=g, pc=pc, bi=bi: e.matmul(pb[bi][0:4, pc * 256: pc * 256 + 255], lhsT=tabb[:, 4 * g:4 * g + 4],
                                                                         rhs=ohvb.v[:, g * 2 + pc, 0:255], start=(pc == 0), stop=True, skip_group_check=True),
                         reads=["tabb", ohvb.key()], writes=[PB(bi)])
                for pc in range(2):
                    ACT(lambda e, g=g, pc=pc, bi=bi: e.activation(out=evec_sb.v[:, g * 2 + pc, 0:255], in_=pb[bi][0:4, pc * 256: pc * 256 + 255], func=AF.Exp),
                        [PB(bi)], [evec_sb.key()])
            S.op("sync", lambda e: e.dma_start(out=evecD, in_=evec_sb.v), reads=[evec_sb.key()], writes=["evecD"], dma_key="evecD")
            for hg in range(4):
                for g in range(3):
                    for pc in range(2):
                        ri_ = rot("Rh", 4)
                        src = bass.AP(tensor=evecD.tensor, offset=(hg * 6 + g * 2 + pc) * 256, ap=[[1, 128], [1, 128]])
                        S.op("sync", lambda e, ri_=ri_, src=src: e.dma_start(out=Rh[ri_].v, in_=src), reads=["evecD"], writes=[Rh[ri_].key()], dma_key=("Rh", ri_))
                        bi = rot("pbs", 2)
                        V(lambda e, ri_=ri_: e.tensor_copy(out=Rhb[ri_].v, in_=Rh[ri_].v), [Rh[ri_].key()], [Rhb[ri_].key()])
                        S.op("tensor", lambda e, ri_=ri_, bi=bi: e.matmul(pb[bi][:, 0:128], lhsT=flipb[:], rhs=Rhb[ri_].v, start=True, stop=True),
                             reads=["flipb", Rhb[ri_].key()], writes=[PB(bi)])
                        if g < 2:
                            V(lambda e, g=g, hg=hg, pc=pc, bi=bi: e.tensor_copy(out=Ecat[g][hg][:, pc * 128:(pc + 1) * 128], in_=pb[bi][:, 0:128]),
                              [PB(bi)], [("E", g, hg)])
                        else:
                            V(lambda e, hg=hg, pc=pc, bi=bi: e.tensor_copy(
                                out=Ecat[2][hg][:].rearrange("k (q c) -> k q c", q=4)[:, :, pc * 32:(pc + 1) * 32],
                                in_=pb[bi][:, 0:128].rearrange("k (q c) -> k q c", q=4)), [PB(bi)], [("E", 2, hg)])
            S.op("gpsimd", lambda e: e.memset(Scarry[:], 0.0), writes=["Scarry"])
            CP("setup_E")

        def quarter_sq(s, qq):
            S.op("scalar", lambda e: e.activation(out=junkq[:], in_=xt[s][:, qq * 256:(qq + 1) * 256], func=AF.Square,
                                                   accum_out=ssq[:, s * 4 + qq:s * 4 + qq + 1]),
                 reads=[("xt", s)], writes=["junkq", ("ssq", s, qq)])

        SSQ_ALL = [("ssq", s_, q_) for s_ in range(NS) for q_ in range(4)]

        def norm_transpose(gname, quarters=True):
            if quarters:
                S.op("vector", lambda e: e.tensor_reduce(out=ss[:, 0:NS], in_=ssq[:].rearrange("p (s q) -> p s q", q=4), axis=mybir.AxisListType.X, op=ALU.add),
                     reads=SSQ_ALL, writes=[("ss", s_) for s_ in range(NS)])
            else:
                for s in range(NS):
                    S.op("scalar", lambda e, s=s: e.activation(out=junk[:], in_=xt[s][:], func=AF.Square, accum_out=ss[:, s:s + 1]),
                         reads=[("xt", s)], writes=["junk", ("ss", s)])
            S.op("scalar", lambda e: e.activation(out=rstd[:, 0:NS], in_=ss[:, 0:NS], func=AF.Sqrt, scale=1.0 / D, bias=EPS),
                 reads=[("ss", s_) for s_ in range(NS)], writes=[("rstd", s_) for s_ in range(NS)])
            S.op("vector", lambda e: e.reciprocal(out=rstd[:, 0:NS], in_=rstd[:, 0:NS]),
                 reads=[("rstd", s_) for s_ in range(NS)], writes=[("rstd", s_) for s_ in range(NS)])
            xbuf = [(xn[0], ("xn", 0)), (junk, "junk")]

            def emit_xn(s):
                xb, xk = xbuf[s % 2]
                S.op("scalar", lambda e: e.activation(out=xb[:], in_=xt[s][:], func=AF.Copy, scale=rstd[:, s:s + 1]),
                     reads=[("xt", s), ("rstd", s)], writes=[xk])

            emit_xn(0)
            for s in range(NS):
                if s + 1 < NS:
                    emit_xn(s + 1)
                xb, xk = xbuf[s % 2]
                pi = 6 + rot("pT", 2)
                pTv = pb[pi].bitcast(BF16)
                for k in range(8):
                    S.op("tensor", lambda e, k=k, xb=xb, pTv=pTv: e.transpose(pTv[:, k * 128:(k + 1) * 128], xb[:, k * 128:(k + 1) * 128], idb[:]),
                         reads=[xk, "idb"], writes=[PB(pi)])
                S.op("vector", lambda e, s=s, pTv=pTv: e.tensor_tensor(out=xT[:, :, s * 128:(s + 1) * 128], in0=pTv.rearrange("p (k c) -> p k c", k=8),
                                                                      in1=gT[gname][:].unsqueeze(2).to_broadcast([128, 8, 128]), op=ALU.mult),
                     reads=[PB(pi), ("gT", gname)], writes=[("xT", s)])

        XT_ALL = [("xT", s) for s in range(NS)]

        def ffn(f):
            norm_transpose(f, quarters=(f != "ffn1"))
            wdv = Wd[f].rearrange("(j p) n -> p j n", p=128)

            def load_wd(qq):
                si = qq % 2
                S.op("sync", lambda e, qq=qq, si=si: e.dma_start(out=wd_sb[si].v, in_=wdv[:, :, qq * 256:(qq + 1) * 256]),
                     reads=[("Wd", f, j_) for j_ in range(NJ)], writes=[wd_sb[si].key()], dma_key=("wd_sb", si))

            def load_wgu(j):
                si = j % 3
                S.op("sync", lambda e, j=j, si=si: e.dma_start(out=wgu_sb[si].v, in_=Wgu[f][j]),
                     reads=[("Wgu", f, j)], writes=[wgu_sb[si].key()], dma_key=("wgu_sb", si))

            load_wgu(0)
            load_wgu(1)
            load_wd(0)
            load_wd(1)
            for j in range(NJ):
                if j + 2 < NJ:
                    load_wgu(j + 2)
                si = j % 3
                gi = rot("pG", 2)
                pG, pU = pb[gi], pb[2 + gi]
                for k in range(8):
                    S.op("tensor", lambda e, k=k, si=si, pG=pG: e.matmul(pG, lhsT=wgu_sb[si].v[:, k, 0:128], rhs=xT[:, k, :], start=(k == 0), stop=(k == 7)),
                         reads=[wgu_sb[si].key()] + XT_ALL, writes=[PB(gi)])
                for k in range(8):
                    S.op("tensor", lambda e, k=k, si=si, pU=pU: e.matmul(pU, lhsT=wgu_sb[si].v[:, k, 128:256], rhs=xT[:, k, :], start=(k == 0), stop=(k == 7)),
                         reads=[wgu_sb[si].key()] + XT_ALL, writes=[PB(2 + gi)])
                gg = rot("sg", 2)
                S.op("scalar", lambda e, gg=gg, pG=pG: e.activation(out=sg[gg][:], in_=pG, func=AF.Silu),
                     reads=[PB(gi)], writes=[("sg", gg)])
                S.op("vector", lambda e, gg=gg, pU=pU, j=j: e.tensor_tensor(out=hT.v[:, j, :], in0=sg[gg][:], in1=pU, op=ALU.mult),
                     reads=[("sg", gg), PB(2 + gi)], writes=[hT.sub(j, NJ)])
            for qq in range(4):
                si = qq % 2
                for s in range(NS):
                    di = 4 + rot("pD", 2)
                    pD = pb[di][:, 0:256]
                    for j in range(NJ):
                        S.op("tensor", lambda e, j=j, s=s, si=si, pD=pD: e.matmul(pD, lhsT=hT.v[:, j, s * 128:(s + 1) * 128], rhs=wd_sb[si].v[:, j, :],
                                                                                   start=(j == 0), stop=(j == NJ - 1)),
                             reads=[hT.sub(j, NJ), wd_sb[si].key()], writes=[PB(di)])
                    S.op("vector", lambda e, s=s, qq=qq, pD=pD: e.scalar_tensor_tensor(
                        out=xt[s][:, qq * 256:(qq + 1) * 256], in0=pD, scalar=0.5, in1=xt[s][:, qq * 256:(qq + 1) * 256], op0=ALU.mult, op1=ALU.add),
                        reads=[PB(di), ("xt", s)], writes=[("xt", s)])
                    quarter_sq(s, qq)
                if qq + 2 < 4:
                    load_wd(qq + 2)

        def MM(out_ap, lhsT, rhs, start, reads, writes, tp=None, sgc=False):
            kw = {}
            if tp is not None:
                kw["tile_position"] = tp
            kw["skip_group_check"] = True
            S.op("tensor", lambda e: e.matmul(out_ap, lhsT=lhsT, rhs=rhs, start=start, stop=False, **kw), reads=reads, writes=writes)

        def mixer(t, light=False):
            norm_transpose("mix")
            if light and S.enabled:
                for s_ in range(NS):
                    prefetch_x[0](t + 1, s_, "gpsimd")
            nb3 = t // 4
            q3 = t % 4
            CP("normT")
            for c in range(4):
                wi = rot("wu", 2)
                S.op("sync", lambda e, c=c, wi=wi: e.dma_start(out=wu_sb[wi][:], in_=Wu[c]), reads=[("Wu", c)], writes=[("wu_sb", wi)], dma_key=("wu_sb", wi))
                bi = (0, 1, 6, 7)[rot("pbA", 4)]
                for k in range(8):
                    MM(pb[bi], wu_sb[wi][:, k, :], xT[:, k, :], k == 0, [("wu_sb", wi)] + XT_ALL, [PB(bi)])
                S.op("scalar", lambda e, c=c, bi=bi: e.copy(out=uT.v[:, c, :], in_=pb[bi]), reads=[PB(bi)], writes=[uT.sub(c, 4)])
            CP("u_proj")
            firstL = [True] * 4
            for ri in range(2):
                for ct in range(4):
                    a_ = ri * 4 + ct
                    for j in range(Q):
                        for pp in range(4):
                            bi = 2 + pp
                            MM(pb[bi][:, a_ * NCH:(a_ + 1) * NCH], Bt[32 * pp:32 * pp + 32, ct, j, ri, :], uT.v[32 * pp:32 * pp + 32, ct, j::Q], firstL[pp],
                               [("Bt", ct), uT.sub(ct, 4)], [PB(bi)], tp=(32 * pp, 0), sgc=True)
                            firstL[pp] = False
            for pp in range(4):
                bi = 2 + pp
                S.op("vector", lambda e, pp=pp, bi=bi: e.tensor_copy(
                    out=Hs.v[:, 1:NCH + 1, pp:32:4], in_=pb[bi].rearrange("q (a k) -> q k a", a=8)),
                    reads=[PB(bi)], writes=[Hs.sub(k_, NCH + 1) for k_ in range(1, NCH + 1)])
            CP("ssm_L")
            if light:
                tt1 = AT(QT.off, [128, 32, 32], F32)
                tt2 = AT(QT.off + 4096, [128, 32, 32], F32)
                HS_L = [Hs.sub(k_, NCH + 1) for k_ in range(1, NCH + 1)]
                S.op("gpsimd", lambda e: e.tensor_copy(out=Hs.v[:, 1:NCH + 1, 32:48], in_=Hs.v[:, 1:NCH + 1, 0:16]), reads=HS_L, writes=HS_L)
                for l in range(1, 7):
                    step, half = 2 ** l, 2 ** (l - 1)
                    n = NCH // step
                    left = Hs.v[:, half:NCH + 1:step, :]
                    right = Hs.v[:, step:NCH + 1:step, :]
                    arb = APR[:, l - 1, :].unsqueeze(1).to_broadcast([128, n, 32])
                    aib = API[:, l - 1, :].unsqueeze(1).to_broadcast([128, n, 32])
                    S.op("gpsimd", lambda e, left=left, arb=arb, n=n: e.tensor_tensor(out=tt1.v[:, 0:n, :], in0=left[:, :, 0:32], in1=arb, op=ALU.mult),
                         reads=HS_L + [("APW", l - 1)], writes=[tt1.key()])
                    S.op("gpsimd", lambda e, left=left, aib=aib, n=n: e.tensor_tensor(out=tt2.v[:, 0:n, :], in0=left[:, :, 16:48], in1=aib, op=ALU.mult),
                         reads=HS_L + [("APW", l - 1)], writes=[tt2.key()])
                    S.op("gpsimd", lambda e, n=n: e.tensor_tensor(out=tt1.v[:, 0:n, :], in0=tt1.v[:, 0:n, :], in1=tt2.v[:, 0:n, :], op=ALU.add),
                         reads=[tt1.key(), tt2.key()], writes=[tt1.key()])
                    S.op("gpsimd", lambda e, right=right, n=n: e.tensor_tensor(out=right[:, :, 0:32], in0=right[:, :, 0:32], in1=tt1.v[:, 0:n, :], op=ALU.add),
                         reads=HS_L + [tt1.key()], writes=HS_L)
                    S.op("gpsimd", lambda e, right=right: e.tensor_copy(out=right[:, :, 32:48], in_=right[:, :, 0:16]), reads=HS_L, writes=HS_L)
                a_ = rot("rt", 2)
                S.op("gpsimd", lambda e, a_=a_: e.tensor_tensor(out=rt1[a_][:], in0=Scarry[:, 0:32], in1=APR[:, 6, :], op=ALU.mult), reads=["Scarry", ("APW", 6)], writes=[("rt1", a_)])
                S.op("gpsimd", lambda e, a_=a_: e.tensor_tensor(out=rt2[a_][:], in0=Scarry[:, 16:48], in1=API[:, 6, :], op=ALU.mult), reads=["Scarry", ("APW", 6)], writes=[("rt2", a_)])
                S.op("gpsimd", lambda e, a_=a_: e.tensor_tensor(out=rt1[a_][:], in0=rt1[a_][:], in1=rt2[a_][:], op=ALU.add), reads=[("rt1", a_), ("rt2", a_)], writes=[("rt1", a_)])
                S.op("gpsimd", lambda e, a_=a_: e.tensor_tensor(out=Scarry[:, 0:32], in0=rt1[a_][:], in1=Hs.v[:, NCH, 0:32], op=ALU.add), reads=[("rt1", a_)] + HS_L, writes=["Scarry"])
                S.op("gpsimd", lambda e: e.tensor_copy(out=Scarry[:, 32:48], in_=Scarry[:, 0:16]), reads=["Scarry"], writes=["Scarry"])
            else:
                HS_L = [Hs.sub(k_, NCH + 1) for k_ in range(1, NCH + 1)]
                HS_A = [Hs.sub(k_, NCH + 1) for k_ in range(0, NCH + 1)]
                G = lambda fn, reads, writes: S.op("gpsimd", fn, reads=reads, writes=writes)
                G(lambda e: e.tensor_copy(out=Hs.v[:, 0, :], in_=Scarry[:]), ["Scarry"], [Hs.sub(0, NCH + 1)])
                G(lambda e: e.tensor_copy(out=Hs.v[:, 1:NCH + 1, 32:48], in_=Hs.v[:, 1:NCH + 1, 0:16]), HS_L, HS_L)
                ar8 = AR2[:].unsqueeze(1).to_broadcast([128, 8, 32])
                ai8 = AI2[:].unsqueeze(1).to_broadcast([128, 8, 32])
                for r in range(1, 8):
                    cur = Hs.v[:, r + 1:NCH + 1:8, :]
                    prev = Hs.v[:, r:NCH:8, :]
                    G(lambda e, prev=prev: e.tensor_tensor(out=bs1.v, in0=prev[:, :, 0:32], in1=ar8, op=ALU.mult), HS_L + ["AR2"], [bs1.key()])
                    G(lambda e, prev=prev: e.tensor_tensor(out=bs2.v, in0=prev[:, :, 16:48], in1=ai8, op=ALU.mult), HS_L + ["AI2"], [bs2.key()])
                    G(lambda e: e.tensor_tensor(out=bs1.v, in0=bs1.v, in1=bs2.v, op=ALU.add), [bs1.key(), bs2.key()], [bs1.key()])
                    G(lambda e, cur=cur: e.tensor_tensor(out=cur[:, :, 0:32], in0=cur[:, :, 0:32], in1=bs1.v, op=ALU.add), HS_L + [bs1.key()], HS_L)
                    G(lambda e, cur=cur: e.tensor_copy(out=cur[:, :, 32:48], in_=cur[:, :, 0:16]), HS_L, HS_L)
                for b in range(8):
                    a = rot("rt", 2)
                    sp_, sc_ = 8 * b, 8 * b + 8
                    G(lambda e, a=a, sp_=sp_: e.tensor_tensor(out=rt1[a][:], in0=Hs.v[:, sp_, 0:32], in1=APR[:, 3, :], op=ALU.mult), HS_A + [("APW", 3)], [("rt1", a)])
                    G(lambda e, a=a, sp_=sp_: e.tensor_tensor(out=rt2[a][:], in0=Hs.v[:, sp_, 16:48], in1=API[:, 3, :], op=ALU.mult), HS_A + [("APW", 3)], [("rt2", a)])
                    G(lambda e, a=a: e.tensor_tensor(out=rt1[a][:], in0=rt1[a][:], in1=rt2[a][:], op=ALU.add), [("rt1", a), ("rt2", a)], [("rt1", a)])
                    G(lambda e, a=a, sc_=sc_: e.tensor_tensor(out=Hs.v[:, sc_, 0:32], in0=Hs.v[:, sc_, 0:32], in1=rt1[a][:], op=ALU.add), HS_A + [("rt1", a)], HS_A)
                    G(lambda e, sc_=sc_: e.tensor_copy(out=Hs.v[:, sc_, 32:48], in_=Hs.v[:, sc_, 0:16]), HS_A, HS_A)
                tprev = Hs.v[:, 0:NCH:8, :]
                for r in range(7):
                    cur = Hs.v[:, r + 1:NCH + 1:8, :]
                    arr = APR3[:, r, :].unsqueeze(1).to_broadcast([128, 8, 32])
                    aii = API3[:, r, :].unsqueeze(1).to_broadcast([128, 8, 32])
                    G(lambda e, arr=arr: e.tensor_tensor(out=bs1.v, in0=tprev[:, :, 0:32], in1=arr, op=ALU.mult), HS_A + [("APW3", r)], [bs1.key()])
                    G(lambda e, aii=aii: e.tensor_tensor(out=bs2.v, in0=tprev[:, :, 16:48], in1=aii, op=ALU.mult), HS_A + [("APW3", r)], [bs2.key()])
                    G(lambda e: e.tensor_tensor(out=bs1.v, in0=bs1.v, in1=bs2.v, op=ALU.add), [bs1.key(), bs2.key()], [bs1.key()])
                    G(lambda e, cur=cur: e.tensor_tensor(out=cur[:, :, 0:32], in0=cur[:, :, 0:32], in1=bs1.v, op=ALU.add), HS_A + [bs1.key()], HS_A)
            CP("recur")
            if light:
                need_g = set()
                if t >= n_prefix - 4:
                    need_g.add(2)
                if t == n_prefix - 1:
                    need_g.update((0, 1))
            else:
                need_g = {0, 1, 2}
            for c in range(12):
                if c < 6 and light:
                    continue
                if c >= 6 and (c - 6) // 2 not in need_g:
                    continue
                wi = rot("wqk", 2)
                S.op("sync", lambda e, c=c, wi=wi: e.dma_start(out=wqk_sb[wi].v, in_=Wqk[c]), reads=[("Wqk", c)], writes=[wqk_sb[wi].key()], dma_key=("wqk_sb", wi))
                bi = (0, 1, 6, 7)[rot("pbA", 4)]
                for k in range(8):
                    MM(pb[bi], wqk_sb[wi].v[:, k, :], xT[:, k, :], k == 0, [wqk_sb[wi].key()] + XT_ALL, [PB(bi)])
                if c < 6:
                    S.op("scalar", lambda e, c=c, bi=bi: e.mul(out=QT.v[:, c, :], in_=pb[bi], mul=0.125), reads=[PB(bi)], writes=[QT.sub(c, 6)])
                else:
                    g, sp = (c - 6) // 2, (c - 6) % 2
                    if g < 2:
                        off, slot = (t % 2) * T, t % 2
                    else:
                        off, slot = (nb3 % 2) * 2048 + q3 * T, (nb3 % 2) * 4 + q3
                    S.op("vector", lambda e, g=g, sp=sp, off=off, bi=bi: e.tensor_copy(out=KT[g][sp][:, off:off + T], in_=pb[bi]),
                         reads=[PB(bi)], writes=[("KT", g, sp, slot)])
            CP("qk")
            for g in range(3):
                if g not in need_g:
                    continue
                wi = rot("wv", 2)
                S.op("sync", lambda e, g=g, wi=wi: e.dma_start(out=wv_sb[wi].v, in_=Wv[g]), reads=[("Wv", g)], writes=[wv_sb[wi].key()], dma_key=("wv_sb", wi))
                if g == 0:
                    for b in range(4):
                        bi = 6 + rot("pbV", 2)
                        for k in range(8):
                            MM(pb[bi][:, 0:256], xT[:, k, b * 128:(b + 1) * 128], wv_sb[wi].v[:, k, :], k == 0, [wv_sb[wi].key()] + XT_ALL, [PB(bi)])
                        vi = (4 * t + b) % 8
                        S.op("scalar", lambda e, bi=bi, vi=vi: e.copy(out=VR[0][:, vi, :], in_=pb[bi][:, 0:256]), reads=[PB(bi)], writes=[("V", 0, vi)])
                elif g == 1:
                    for r in range(4):
                        bi = 6 + rot("pbV", 2)
                        for k in range(8):
                            MM(pb[bi][:, 0:256], xT[:, k, r::4], wv_sb[wi].v[:, k, :], k == 0, [wv_sb[wi].key()] + XT_ALL, [PB(bi)])
                        vi = (t % 2) * 4 + r
                        S.op("scalar", lambda e, bi=bi, vi=vi: e.copy(out=VR[1][:, vi, :], in_=pb[bi][:, 0:256]), reads=[PB(bi)], writes=[("V", 1, vi)])
                else:
                    for r2 in range(8):
                        bi = 6 + rot("pbV", 2)
                        for rr in range(2):
                            r = 2 * r2 + rr
                            for k in range(8):
                                MM(pb[bi][32 * q3:32 * q3 + 32, rr * 256:(rr + 1) * 256], xT[:, k, r::16], wv_sb[wi].v[:, k, :], (k == 0 and rr == 0),
                                   [wv_sb[wi].key()] + XT_ALL, [PB(bi)], tp=(0, 32 * q3), sgc=True)
                        vi = (nb3 % 2) * 16 + 2 * r2
                        S.op("scalar", lambda e, bi=bi, vi=vi: e.copy(out=VR[2][32 * q3:32 * q3 + 32, vi:vi + 2, :],
                                                                     in_=pb[bi][32 * q3:32 * q3 + 32, :].rearrange("k (r c) -> k r c", r=2)),
                             reads=[PB(bi)], writes=[("V", 2, vi), ("V", 2, vi + 1)])
            CP("v")
            if light:
                return
            started = set()

            def acc(bank, half, cols_ap, lhsT, rhs, reads):
                first = (bank, half) not in started
                started.add((bank, half))
                MM(cols_ap, lhsT, rhs, first, reads, [PB(bank)], tp=(0, 64 * half), sgc=True)

            groups = []
            for g in range(3):
                for hg in range(4):
                    sp, hh = hg // 2, hg % 2
                    qv = QT.v[64 * hh:64 * hh + 64, 2 * g + sp, :]
                    kt = KT[g][sp]
                    units = []
                    if g == 0:
                        for b in range(4):
                            qs = slice(b * 128, (b + 1) * 128)
                            blk = 4 * t + b
                            for pc in range(2):
                                kb = blk - 1 + pc
                                if kb < 0:
                                    units.append(None)
                                    continue
                                ko = (kb * 128) % (2 * T)
                                units.append((qs, kt[64 * hh:64 * hh + 64, ko:ko + 128], ("V", 0, kb % 8), VR[0][:, kb % 8, hg * 64:(hg + 1) * 64],
                                              [("KT", 0, sp, (kb // 4) % 2)], kb // 4))
                        W = 128
                    elif g == 1:
                        for r in range(4):
                            qs = slice(r, T, 4)
                            for pc in range(2):
                                tt = t - 1 + pc
                                if tt < 0:
                                    units.append(None)
                                    continue
                                ko = (tt % 2) * T
                                vi = (tt % 2) * 4 + r
                                units.append((qs, kt[64 * hh:64 * hh + 64, ko + r:ko + T:4], ("V", 1, vi), VR[1][:, vi, hg * 64:(hg + 1) * 64], [("KT", 1, sp, tt % 2)], tt))
                        W = 128
                    else:
                        for r in range(16):
                            qs = slice(r, T, 16)
                            for pc in range(2):
                                nb = nb3 - 1 + pc
                                if nb < 0:
                                    units.append(None)
                                    continue
                                ko = (nb % 2) * 2048
                                vi = (nb % 2) * 16 + r
                                units.append((qs, kt[64 * hh:64 * hh + 64, ko + r:ko + 2048:16], ("V", 2, vi), VR[2][:, vi, hg * 64:(hg + 1) * 64],
                                              [("KT", 2, sp, (nb % 2) * 4 + s_) for s_ in range(4)], nb * 4))
                        W = 32
                    per_bank = T // W
                    for u0 in range(0, len(units), per_bank):
                        grp = units[u0:u0 + per_bank]
                        if all(u is None for u in grp):
                            continue
                        groups.append((g, hg, sp, hh, qv, W, grp))

            SB = (0, 1, 6, 7)

            def emit_S(i):
                g, hg, sp, hh, qv, W, grp = groups[i]
                bi = SB[i % 4]
                for ui, u in enumerate(grp):
                    if u is None:
                        continue
                    qs, kap, vkey, vap, kkeys, ktile = u
                    MM(pb[bi][:, ui * W:(ui + 1) * W], kap, qv[:, qs], True, kkeys + [QT.sub(2 * g + sp, 6)], [PB(bi)], tp=(64 * hh, 0), sgc=True)

            def emit_E(i):
                g, hg, sp, hh, qv, W, grp = groups[i]
                bi, ei, pi_ = SB[i % 4], i % 2, i % 2
                S.op("scalar", lambda e: e.activation(out=sg[ei][:], in_=pb[bi], func=AF.Exp), reads=[PB(bi)], writes=[("sg", ei)])
                if g < 2:
                    ein = Ecat[g][hg][:].unsqueeze(1).to_broadcast([128, 2, 256])
                    S.op("vector", lambda e: e.tensor_tensor(
                        out=PT[pi_].v.rearrange("k (a c) -> k a c", a=2), in0=sg[ei][:].rearrange("k (a c) -> k a c", a=2), in1=ein, op=ALU.mult),
                        reads=[("sg", ei), ("E", g, hg)], writes=[PT[pi_].key()])
                else:
                    ein = Ecat[2][hg][:, q3 * 64:(q3 + 1) * 64].unsqueeze(1).to_broadcast([128, 8, 64])
                    S.op("vector", lambda e: e.tensor_tensor(
                        out=PT[pi_].v.rearrange("k (a c) -> k a c", a=8), in0=sg[ei][:].rearrange("k (a c) -> k a c", a=8), in1=ein, op=ALU.mult),
                        reads=[("sg", ei), ("E", 2, hg)], writes=[PT[pi_].key()])

            def emit_PV(i):
                g, hg, sp, hh, qv, W, grp = groups[i]
                pi_ = i % 2
                for ui, u in enumerate(grp):
                    if u is None:
                        continue
                    qs, kap, vkey, vap, kkeys, ktile = u
                    prhs = PT[pi_].v[:, ui * W:(ui + 1) * W]
                    acc(2 + sp, hh, pb[2 + sp][64 * hh:64 * hh + 64, qs], vap, prhs, [vkey, PT[pi_].key()])
                    if ktile < n_prefix:
                        acc(4 + sp, hh, pb[4 + sp][64 * hh:64 * hh + 64, qs], valid_bf[:], prhs, ["valid", PT[pi_].key()])
                    else:
                        acc(4 + sp, hh, pb[4 + sp][64 * hh:64 * hh + 64, qs], ones_bf[:], prhs, ["ones", PT[pi_].key()])

            ng = len(groups)
            for i in range(min(2, ng)):
                emit_S(i)
            emit_E(0)
            for i in range(ng):
                if i + 2 < ng:
                    emit_S(i + 2)
                if i + 1 < ng:
                    emit_E(i + 1)
                emit_PV(i)
            for sp in range(2):
                S.op("vector", lambda e, sp=sp: e.reciprocal(out=rden[sp][:], in_=pb[4 + sp]), reads=[PB(4 + sp)], writes=[("rden", sp)])
                S.op("vector", lambda e, sp=sp: e.tensor_tensor(out=oT.v[:, sp, :], in0=pb[2 + sp], in1=rden[sp][:], op=ALU.mult),
                     reads=[PB(2 + sp), ("rden", sp)], writes=[oT.sub(sp, 2)])
            CP("attn")
            HS_ALL = [Hs.sub(k_, NCH + 1) for k_ in range(NCH)]
            for ri in range(2):
                S.op("scalar", lambda e, ri=ri: e.copy(out=Sb[ri].v, in_=Hs.v[:, 0:NCH, ri * 16:(ri + 1) * 16].rearrange("q k p -> q p k")),
                     reads=HS_ALL, writes=[Sb[ri].key()])
            S.op("gpsimd", lambda e: e.tensor_copy(out=Scarry[:], in_=Hs.v[:, NCH, :]), reads=[Hs.sub(NCH, NCH + 1)], writes=["Scarry"])
            YB = (6, 7, 0, 1)
            for ct in range(4):
                bi = YB[ct]
                first = True
                for tau in range(Q):
                    for l in range(tau + 1):
                        MM(pb[bi][:, tau::Q], Gt[:, ct, l, :], uT.v[:, ct, (tau - l)::Q], first, [("Gt", ct, l), uT.sub(ct, 4)], [PB(bi)], sgc=True)
                        first = False
            for ct in range(4):
                bi = YB[ct]
                for j in range(Q):
                    for ri in range(2):
                        for pp in range(4):
                            p = 4 * ct + pp
                            MM(pb[bi][32 * pp:32 * pp + 32, j::Q], Ct[:, p, j, ri, :], Sb[ri].v[:, p, :], False, ["Ct", Sb[ri].key()], [PB(bi)], tp=(0, 32 * pp), sgc=True)
                yi = rot("ysb", 2)
                S.op("scalar", lambda e, bi=bi, yi=yi: e.copy(out=ysb[yi].v, in_=pb[bi]), reads=[PB(bi)], writes=[ysb[yi].key()])
                S.op("vector", lambda e, yi=yi: e.tensor_tensor(out=rden[yi][:], in0=ysb[yi].v, in1=ysb[yi].v, op=ALU.mult), reads=[ysb[yi].key()], writes=[("rden", yi)])
                S.op("vector", lambda e, yi=yi: e.tensor_scalar(out=rden[yi][:], in0=rden[yi][:], scalar1=0.044715, scalar2=1.0, op0=ALU.mult, op1=ALU.add),
                     reads=[("rden", yi)], writes=[("rden", yi)])
                S.op("vector", lambda e, yi=yi: e.tensor_tensor(out=rden[yi][:], in0=rden[yi][:], in1=ysb[yi].v, op=ALU.mult), reads=[("rden", yi), ysb[yi].key()], writes=[("rden", yi)])
                S.op("scalar", lambda e, yi=yi: e.activation(out=rden[yi][:], in_=rden[yi][:], func=AF.Sigmoid, scale=2.0 * math.sqrt(2.0 / math.pi)),
                     reads=[("rden", yi)], writes=[("rden", yi)])
                S.op("vector", lambda e, yi=yi, ct=ct: e.tensor_tensor(out=geT.v[:, ct, :], in0=rden[yi][:], in1=ysb[yi].v, op=ALU.mult),
                     reads=[("rden", yi), ysb[yi].key()], writes=[geT.sub(ct, 4)])
            CP("ssm_y")
            GE_ALL = [geT.sub(c_, 4) for c_ in range(4)]
            for f in range(4):
                wa, wb = rot("wglu", 4), rot("wglu", 4)
                S.op("sync", lambda e, f=f, wa=wa: e.dma_start(out=wglu_sb[wa].v, in_=Wglu[f]), reads=[("Wglu", f)], writes=[wglu_sb[wa].key()], dma_key=("wglu_sb", wa))
                S.op("sync", lambda e, f=f, wb=wb: e.dma_start(out=wglu_sb[wb].v, in_=Wglu[4 + f]), reads=[("Wglu", 4 + f)], writes=[wglu_sb[wb].key()], dma_key=("wglu_sb", wb))
                ba, bb_ = (4, 5) if f % 2 == 0 else (2, 3)
                for k in range(4):
                    MM(pb[ba], wglu_sb[wa].v[:, k, :], geT.v[:, k, :], k == 0, [wglu_sb[wa].key(), geT.sub(k, 4)], [PB(ba)])
                for k in range(4):
                    MM(pb[bb_], wglu_sb[wb].v[:, k, :], geT.v[:, k, :], k == 0, [wglu_sb[wb].key(), geT.sub(k, 4)], [PB(bb_)])
                gi_ = rot("sgE", 2)
                S.op("scalar", lambda e, gi_=gi_, bb_=bb_: e.activation(out=sg[gi_][:], in_=pb[bb_], func=AF.Sigmoid), reads=[PB(bb_)], writes=[("sg", gi_)])
                S.op("vector", lambda e, gi_=gi_, ba=ba, f=f: e.tensor_tensor(out=ys2T.v[:, f, :], in0=sg[gi_][:], in1=pb[ba], op=ALU.mult),
                     reads=[("sg", gi_), PB(ba)], writes=[ys2T.sub(f, 4)])
            CP("glu")
            YS_ALL = [ys2T.sub(f_, 4) for f_ in range(4)]
            for dt_ in range(8):
                w1, w2 = rot("wab", 2), rot("wsb", 2)
                w3, w4 = rot("wg", 4), rot("wg", 4)
                S.op("sync", lambda e, dt_=dt_, w1=w1: e.dma_start(out=wab_sb[w1][:], in_=Wab[dt_]), reads=[("Wab", dt_)], writes=[("wab_sb", w1)], dma_key=("wab_sb", w1))
                S.op("sync", lambda e, dt_=dt_, w2=w2: e.dma_start(out=wsb_sb[w2][:], in_=Wsb[dt_]), reads=[("Wsb", dt_)], writes=[("wsb_sb", w2)], dma_key=("wsb_sb", w2))
                S.op("sync", lambda e, dt_=dt_, w3=w3: e.dma_start(out=wg_sb[w3].v, in_=Wg[dt_]), reads=[("Wg", dt_)], writes=[wg_sb[w3].key()], dma_key=("wg_sb", w3))
                S.op("sync", lambda e, dt_=dt_, w4=w4: e.dma_start(out=wg_sb[w4].v, in_=Wg[8 + dt_]), reads=[("Wg", 8 + dt_)], writes=[wg_sb[w4].key()], dma_key=("wg_sb", w4))
                b0 = 0 if dt_ % 2 == 0 else 4
                for k in range(2):
                    MM(pb[b0], wab_sb[w1][:, k, :], oT.v[:, k, :], k == 0, [("wab_sb", w1), oT.sub(0, 2), oT.sub(1, 2)], [PB(b0)])
                for k in range(4):
                    MM(pb[b0 + 1], wsb_sb[w2][:, k, :], ys2T.v[:, k, :], k == 0, [("wsb_sb", w2)] + YS_ALL, [PB(b0 + 1)])
                for k in range(8):
                    MM(pb[b0 + 2], wg_sb[w3].v[:, k, :], xT[:, k, :], k == 0, [wg_sb[w3].key()] + XT_ALL, [PB(b0 + 2)])
                for k in range(8):
                    MM(pb[b0 + 3], wg_sb[w4].v[:, k, :], xT[:, k, :], k == 0, [wg_sb[w4].key()] + XT_ALL, [PB(b0 + 3)])
                ya, ys_ = rot("ysb", 2), rot("ysb", 2)
                S.op("scalar", lambda e, dt_=dt_, b0=b0, ya=ya: e.activation(out=ysb[ya].v, in_=pb[b0 + 2], func=AF.Sigmoid, bias=gbT[:, dt_:dt_ + 1]),
                     reads=[PB(b0 + 2), "gbT"], writes=[ysb[ya].key()])
                S.op("scalar", lambda e, dt_=dt_, b0=b0, ys_=ys_: e.activation(out=ysb[ys_].v, in_=pb[b0 + 3], func=AF.Sigmoid, bias=gbT[:, 8 + dt_:9 + dt_]),
                     reads=[PB(b0 + 3), "gbT"], writes=[ysb[ys_].key()])
                S.op("vector", lambda e, b0=b0, ya=ya: e.tensor_tensor(out=ysb[ya].v, in0=ysb[ya].v, in1=pb[b0], op=ALU.mult), reads=[ysb[ya].key(), PB(b0)], writes=[ysb[ya].key()])
                S.op("vector", lambda e, b0=b0, ys_=ys_: e.tensor_tensor(out=ysb[ys_].v, in0=ysb[ys_].v, in1=pb[b0 + 1], op=ALU.mult), reads=[ysb[ys_].key(), PB(b0 + 1)], writes=[ysb[ys_].key()])
                S.op("vector", lambda e, dt_=dt_, ya=ya, ys_=ys_: e.tensor_tensor(out=mT.v[:, dt_, :], in0=ysb[ya].v, in1=ysb[ys_].v, op=ALU.add),
                     reads=[ysb[ya].key(), ysb[ys_].key()], writes=[mT.sub(dt_, 8)])
            CP("merge")
            MT_ALL = [mT.sub(k_, 8) for k_ in range(8)]
            wov = Wo.rearrange("(k p) n -> p k n", p=128)
            for qq in range(4):
                wi = rot("wo", 2)
                S.op("sync", lambda e, qq=qq, wi=wi: e.dma_start(out=wo_sb[wi].v, in_=wov[:, :, qq * 256:(qq + 1) * 256]),
                     reads=[("Wo", k_) for k_ in range(8)], writes=[wo_sb[wi].key()], dma_key=("wo_sb", wi))
                for s in range(NS):
                    bi = rot("pbO", 4)
                    for k in range(8):
                        MM(pb[bi][:, 0:256], mT.v[:, k, s * 128:(s + 1) * 128], wo_sb[wi].v[:, k, :], k == 0, MT_ALL + [wo_sb[wi].key()], [PB(bi)])
                    S.op("vector", lambda e, s=s, qq=qq, bi=bi: e.tensor_tensor(out=xt[s][:, qq * 256:(qq + 1) * 256], in0=pb[bi][:, 0:256],
                                                                                in1=xt[s][:, qq * 256:(qq + 1) * 256], op=ALU.add),
                         reads=[PB(bi), ("xt", s)], writes=[("xt", s)])
                    quarter_sq(s, qq)

        x_loaded = set()

        def load_x(t, s, eng="sync"):
            if t >= n_tiles or (t, s) in x_loaded:
                return
            x_loaded.add((t, s))
            r0 = t * T + s * 128
            S.op(eng, lambda e: e.dma_start(out=xt[s][:], in_=I["x"][r0:r0 + 128, :]), writes=[("xt", s)], dma_key=("xl" + eng[0], s))

        prefetch_x[0] = load_x
        for t in range(n_tiles):
            light = t < n_prefix
            cur_tile[0] = t
            for s in range(NS):
                load_x(t, s)
            if "ffn1" in stages:
                ffn("ffn1")
            if "mix" in stages:
                mixer(t, light)
            if light:
                S.enabled = True
                continue
            if "ffn2" in stages:
                ffn("ffn2")
            S.enabled = True
            S.op("sync", lambda e: e.dma_start(out=grow_final.v, in_=I["final_norm"].unsqueeze(0).to_broadcast([128, D])),
                 writes=[grow_final.key()], dma_key="grow_final")
            S.op("vector", lambda e: e.tensor_reduce(out=ss[:, 4:8], in_=ssq[:].rearrange("p (s q) -> p s q", q=4), axis=mybir.AxisListType.X, op=ALU.add),
                 reads=SSQ_ALL, writes=[("ss", 4 + s_) for s_ in range(NS)])
            S.op("scalar", lambda e: e.activation(out=rstd[:, 4:8], in_=ss[:, 4:8], func=AF.Sqrt, scale=1.0 / D, bias=EPS),
                 reads=[("ss", 4 + s_) for s_ in range(NS)], writes=[("rstd", 4 + s_) for s_ in range(NS)])
            S.op("vector", lambda e: e.reciprocal(out=rstd[:, 4:8], in_=rstd[:, 4:8]),
                 reads=[("rstd", 4 + s_) for s_ in range(NS)], writes=[("rstd", 4 + s_) for s_ in range(NS)])
            for s in range(NS):
                col = 4 + s
                S.op("vector", lambda e, s=s, col=col: e.scalar_tensor_tensor(
                    out=ot[s].v, in0=xt[s][:], scalar=rstd[:, col:col + 1], in1=grow_final.v, op0=ALU.mult, op1=ALU.mult),
                    reads=[("xt", s), ("rstd", col), grow_final.key()], writes=[ot[s].key()])
                r0 = (t - n_prefix) * T + s * 128
                S.op("sync", lambda e, s=s, r0=r0: e.dma_start(out=out[r0:r0 + 128, :], in_=ot[s].v),
                     reads=[ot[s].key()], writes=[("out", t, s)], dma_key=("st", s))
                load_x(t + 1, s)
        S.op("sync", lambda e: e.dma_start(out=ss[:, 0:1], in_=I["final_norm"][0:128].unsqueeze(1)), reads=[("out", t_, s_) for t_ in range(n_prefix, n_tiles) for s_ in range(NS)],
             writes=[("ss", 0)], dma_key="fin")
        S.emit()
    return nc


def kernel(**inputs):
    n_cores = 8
    nc = build_nc()
    shared = host_consts()
    shared.update(host_weight_layouts(inputs))
    for name, shape in INPUT_SPECS:
        if name == "x" or name.startswith("c_") or name.startswith("h_"):
            continue
        shared[name] = np.ascontiguousarray(np.asarray(inputs[name], dtype=np.float32))
    x = np.asarray(inputs["x"], dtype=np.float32)
    half = x.shape[1] // 2
    in_maps = []
    for c in range(n_cores):
        b, h = c // 2, c % 2
        m = dict(shared)
        xc = np.zeros((2 * half, x.shape[2]), np.float32)
        if h == 1:
            xc[:half] = x[b, :half]
        xc[half:] = x[b, h * half:(h + 1) * half]
        m["x"] = xc
        m["c_valid"] = np.full((128, 64), float(h), np.float32)
        in_maps.append(m)
    res = run_bass_kernel_spmd(nc, in_maps, core_ids=list(range(n_cores)))
    out = np.empty(x.shape, np.float32)
    for c in range(n_cores):
        b, h = c // 2, c % 2
        out[b, h * half:(h + 1) * half] = res.results[c]["out"]
    return out
```
